# Optimizing a Trainium2 kernel written in Bass

```python
import jax, jax.numpy as jnp
from jax import lax
import numpy as np

D_MODEL = 2048
BATCH = 32
SEQ = 256
DEPTH = 1
DEC_BATCH = 8
DEC_SEQ = 1024
PAST_LEN = 512

GRID_W = 64
N_HEADS_A = 8
HEAD_K = 128
HEAD_V = 128
WIDTH_A = N_HEADS_A * HEAD_V
N_GROUPS_B = 8
GROUP_C = 128
WIDTH_B = N_GROUPS_B * GROUP_C
MIX_WIDTH = WIDTH_A + WIDTH_B
SGU_CHUNK = 128
DELTA_CHUNK = 64
SHORT_CONV = 5
FFN_CONV = 3
D_FF = 5632
N_MOD = 6
EPS = 1e-6
QK_WIDTH = N_HEADS_A * HEAD_K
QKV_WIDTH = 2 * QK_WIDTH + WIDTH_A
IN_OFFSETS = (QKV_WIDTH, QKV_WIDTH + WIDTH_A, QKV_WIDTH + WIDTH_A + 2 * N_HEADS_A,
              QKV_WIDTH + WIDTH_A + 4 * N_HEADS_A, QKV_WIDTH + WIDTH_A + 4 * N_HEADS_A + WIDTH_B)
P_IN = QKV_WIDTH + WIDTH_A + 4 * N_HEADS_A + 2 * WIDTH_B

kernel_name = "hybrid_gdn_gmlp_flow_step"


def _rmsnorm(x, w):
    xf = x.astype(jnp.float32)
    y = xf * lax.rsqrt(jnp.mean(xf * xf, axis=-1, keepdims=True) + EPS)
    return (y * w.astype(jnp.float32)).astype(x.dtype)


def _l2norm(x):
    xf = x.astype(jnp.float32)
    return xf * lax.rsqrt(jnp.sum(xf * xf, axis=-1, keepdims=True) + EPS)


def _adaln(cvec, w_ada, b_ada):
    return jax.nn.silu(cvec) @ w_ada + b_ada


def _modulate(h, shift, scale):
    return h * (1 + scale) + shift


def _dwconv(x, w, rows):
    B, T, C = x.shape
    kh, kw = w.shape[0], w.shape[1]
    xg = x.reshape(B, rows, T // rows, C)
    y = lax.conv_general_dilated(
        xg, w.reshape(kh, kw, 1, C).astype(x.dtype), window_strides=(1, 1),
        padding=((kh // 2, kh // 2), (kw // 2, kw // 2)),
        dimension_numbers=('NHWC', 'HWIO', 'NHWC'), feature_group_count=C)
    return y.reshape(B, T, C)


def _gated_delta(q, k, v, g, beta, s0):
    B, T, H, DK = q.shape
    DV = v.shape[-1]
    n = T // DELTA_CHUNK

    def chunks(t):
        t = jnp.moveaxis(t.astype(jnp.float32), 2, 1)
        return t.reshape(B, H, n, DELTA_CHUNK, *t.shape[3:])

    qc = chunks(q) * (DK ** -0.5)
    kc, vc = chunks(k), chunks(v)
    gc = jnp.cumsum(chunks(g), axis=-1)
    bc = chunks(beta)
    idx = jnp.arange(DELTA_CHUNK)
    incl = idx[:, None] >= idx[None, :]
    strict = idx[:, None] > idx[None, :]
    decay = jnp.exp(jnp.where(incl, gc[..., :, None] - gc[..., None, :], -jnp.inf))
    kb = kc * bc[..., None]
    lmat = jnp.where(strict, jnp.einsum('bhnid,bhnjd->bhnij', kb, kc) * decay, 0.0)
    rhs = jnp.concatenate([vc * bc[..., None], kb * jnp.exp(gc)[..., None]], axis=-1)
    sol = lax.linalg.triangular_solve(lmat, rhs, left_side=True, lower=True, unit_diagonal=True)
    u_c, w_c = sol[..., :DV], sol[..., DV:]
    attn = jnp.einsum('bhnid,bhnjd->bhnij', qc, kc) * decay
    q_dec = qc * jnp.exp(gc)[..., None]
    k_dec = kc * jnp.exp(gc[..., -1:] - gc)[..., None]
    g_last = jnp.exp(gc[..., -1])

    def step(s, xs):
        u_i, w_i, a_i, q_i, k_i, gl = xs
        v_new = u_i - jnp.einsum('bhcd,bhde->bhce', w_i, s)
        o_i = jnp.einsum('bhcd,bhde->bhce', q_i, s) + jnp.einsum('bhij,bhje->bhie', a_i, v_new)
        s = s * gl[..., None, None] + jnp.einsum('bhcd,bhce->bhde', k_i, v_new)
        return s, o_i

    xs = tuple(jnp.moveaxis(t, 2, 0) for t in (u_c, w_c, attn, q_dec, k_dec, g_last))
    s_fin, o = lax.scan(step, s0.astype(jnp.float32), xs)
    o = jnp.moveaxis(o, 0, 2).reshape(B, H, T, DV)
    return jnp.moveaxis(o, 1, 2), s_fin


def _layer(x, mod, rows, s0f, s0b, norm1_w, w_in, conv_a_w, a_log, dt_bias, gdn_norm_w,
           sgu_norm_w, sgu_w, sgu_b, w_out, norm2_w, w_up, ffn_conv_w, ffn_conv_b, w_down):
    B, T, _ = x.shape
    H = N_HEADS_A
    shift1, scale1, gate1, shift2, scale2, gate2 = jnp.split(mod[:, None, :], N_MOD, axis=-1)

    h = _modulate(_rmsnorm(x, norm1_w), shift1, scale1)
    proj = h @ w_in
    qkv, g_out, a_raw, b_raw, u, sv = jnp.split(proj, list(IN_OFFSETS), axis=-1)

    qkv = jax.nn.silu(_dwconv(qkv, conv_a_w, 1))
    q, k, v = jnp.split(qkv, [QK_WIDTH, 2 * QK_WIDTH], axis=-1)
    q = _l2norm(q.reshape(B, T, H, HEAD_K))
    k = _l2norm(k.reshape(B, T, H, HEAD_K))
    v = v.reshape(B, T, H, HEAD_V)
    a = a_raw.astype(jnp.float32).reshape(B, T, 2, H)
    glog = -jnp.exp(a_log.astype(jnp.float32)) * jax.nn.softplus(a + dt_bias.astype(jnp.float32))
    beta = jax.nn.sigmoid(b_raw.astype(jnp.float32)).reshape(B, T, 2, H)
    o_f, s_f = _gated_delta(q, k, v, glog[:, :, 0], beta[:, :, 0], s0f)
    rev = lambda t: jnp.flip(t, axis=1)
    o_r, s_b = _gated_delta(rev(q), rev(k), rev(v), rev(glog[:, :, 1]), rev(beta[:, :, 1]), s0b)
    o = o_f + rev(o_r)
    o = _rmsnorm(o, gdn_norm_w) * jax.nn.silu(g_out.astype(jnp.float32).reshape(B, T, H, HEAD_V))
    o_a = o.reshape(B, T, WIDTH_A).astype(x.dtype)

    u = jax.nn.gelu(u)
    sv = _rmsnorm(jax.nn.gelu(sv), sgu_norm_w)
    svc = sv.reshape(B, T // SGU_CHUNK, SGU_CHUNK, N_GROUPS_B, GROUP_C)
    mixed = jnp.einsum('gij,bnjgc->bnigc', sgu_w, svc) + sgu_b.T[None, None, :, :, None]
    o_b = u * mixed.reshape(B, T, WIDTH_B)

    x = x + gate1 * (jnp.concatenate([o_a, o_b], axis=-1) @ w_out)

    h = _modulate(_rmsnorm(x, norm2_w), shift2, scale2)
    up = _dwconv(h @ w_up, ffn_conv_w, rows) + ffn_conv_b
    gt, val = jnp.split(up, 2, axis=-1)
    x = x + gate2 * ((jax.nn.silu(gt) * val) @ w_down)
    return x, s_f, s_b


def setup_inputs(seed: int = 0) -> dict:
    key = jax.random.key(seed)
    ks = jax.random.split(key, 24)
    nrm = lambda k, s, sc: jax.random.normal(k, s, jnp.float32) * sc
    L = DEPTH
    a_val = jax.random.uniform(ks[9], (L, 2, N_HEADS_A), jnp.float32, 1.0, 16.0)
    dt = jnp.exp(jax.random.uniform(ks[10], (L, 2, N_HEADS_A), jnp.float32, np.log(1e-3), np.log(1e-1)))
    return {
        'x_prompt': nrm(ks[0], (BATCH, SEQ, D_MODEL), 1.0),
        'x_sample': nrm(ks[1], (DEC_BATCH, DEC_SEQ, D_MODEL), 1.0),
        'state_fwd': nrm(ks[2], (DEC_BATCH, DEPTH, N_HEADS_A, HEAD_K, HEAD_V), HEAD_K ** -0.5),
        'state_bwd': nrm(ks[3], (DEC_BATCH, DEPTH, N_HEADS_A, HEAD_K, HEAD_V), HEAD_K ** -0.5),
        'c': nrm(ks[4], (DEC_BATCH, D_MODEL), 1.0),
        'c_ctx': nrm(ks[5], (D_MODEL,), 1.0),
        'w_ada': nrm(ks[6], (L, D_MODEL, N_MOD * D_MODEL), 0.5 * D_MODEL ** -0.5),
        'b_ada': nrm(ks[7], (L, N_MOD * D_MODEL), 0.02),
        'norm1_w': 1.0 + nrm(ks[8], (L, D_MODEL), 0.02),
        'w_in': nrm(ks[11], (L, D_MODEL, P_IN), D_MODEL ** -0.5),
        'conv_a_w': nrm(ks[12], (L, 1, SHORT_CONV, QKV_WIDTH), SHORT_CONV ** -0.5),
        'a_log': jnp.log(a_val),
        'dt_bias': dt + jnp.log(-jnp.expm1(-dt)),
        'gdn_norm_w': 1.0 + nrm(ks[13], (L, HEAD_V), 0.02),
        'sgu_norm_w': 1.0 + nrm(ks[14], (L, WIDTH_B), 0.02),
        'sgu_w': nrm(ks[15], (L, N_GROUPS_B, SGU_CHUNK, SGU_CHUNK), SGU_CHUNK ** -0.5),
        'sgu_b': 1.0 + nrm(ks[16], (L, N_GROUPS_B, SGU_CHUNK), 0.02),
        'w_out': nrm(ks[17], (L, MIX_WIDTH, D_MODEL), MIX_WIDTH ** -0.5),
        'norm2_w': 1.0 + nrm(ks[18], (L, D_MODEL), 0.02),
        'w_up': nrm(ks[19], (L, D_MODEL, 2 * D_FF), D_MODEL ** -0.5),
        'ffn_conv_w': nrm(ks[20], (L, FFN_CONV, FFN_CONV, 2 * D_FF), FFN_CONV ** -1.0),
        'ffn_conv_b': nrm(ks[21], (L, 2 * D_FF), 0.02),
        'w_down': nrm(ks[22], (L, D_FF, D_MODEL), D_FF ** -0.5),
        'final_norm_w': 1.0 + nrm(ks[23], (D_MODEL,), 0.02),
    }


def reference(x_prompt, x_sample, state_fwd, state_bwd, c, c_ctx, w_ada, b_ada, norm1_w, w_in,
              conv_a_w, a_log, dt_bias, gdn_norm_w, sgu_norm_w, sgu_w, sgu_b, w_out, norm2_w,
              w_up, ffn_conv_w, ffn_conv_b, w_down, final_norm_w):
    rows = x_sample.shape[1] // GRID_W
    zero_state = jnp.zeros((x_prompt.shape[0], N_HEADS_A, HEAD_K, HEAD_V), jnp.float32)
    xp, xs = x_prompt, x_sample
    new_f, new_b = [], []
    for l in range(DEPTH):
        lw = (norm1_w[l], w_in[l], conv_a_w[l], a_log[l], dt_bias[l], gdn_norm_w[l], sgu_norm_w[l],
              sgu_w[l], sgu_b[l], w_out[l], norm2_w[l], w_up[l], ffn_conv_w[l], ffn_conv_b[l], w_down[l])
        mod_ctx = _adaln(c_ctx[None, :], w_ada[l], b_ada[l])
        mod_lat = _adaln(c, w_ada[l], b_ada[l])
        xp, sf, sb = _layer(xp, mod_ctx, 1, zero_state, zero_state, *lw)
        new_f.append(sf)
        new_b.append(sb)
        xs, _, _ = _layer(xs, mod_lat, rows, state_fwd[:, l], state_bwd[:, l], *lw)
    y_prompt = _rmsnorm(xp, final_norm_w)
    y_sample = _rmsnorm(xs, final_norm_w)
    new_state_fwd = jnp.stack(new_f, axis=1).astype(x_prompt.dtype)
    new_state_bwd = jnp.stack(new_b, axis=1).astype(x_prompt.dtype)
    return (y_prompt, y_sample, new_state_fwd, new_state_bwd)
```

```python
import contextlib
import numpy as np
import concourse.bass as bass
import concourse.mybir as mybir
from concourse.bass_utils import run_bass_kernel_spmd

F32 = mybir.dt.float32
BF16 = mybir.dt.bfloat16
AF = mybir.ActivationFunctionType
ALU = mybir.AluOpType
AX = mybir.AxisListType
_DT_SIZE = {F32: 4, BF16: 2, mybir.dt.int32: 4}


class Op:
    __slots__ = ("eng", "fn", "deps", "dma_key", "dma_val", "milestone", "ms_no")

    def __init__(self, eng, fn, dma_key=None):
        self.eng = eng
        self.fn = fn
        self.deps = set()
        self.dma_key = dma_key
        self.dma_val = 0
        self.milestone = False
        self.ms_no = 0


def _region(ap):
    t = ap.tensor
    aps = ap.ap
    off = int(ap.offset)
    dsz = _DT_SIZE.get(ap.dtype, 4)
    sp = str(ap.space)
    if sp in ("SB", "PSUM"):
        pstep = aps[0][0]
        if pstep == 0:
            row = 1
            for s in list(t.shape)[1:]:
                row *= int(s)
            pstep = row * _DT_SIZE.get(t.dtype, 4) // dsz
        plo = off // pstep
        flo = off % pstep
        phi = plo + aps[0][1]
        span = 1
        for st, cnt in aps[1:]:
            span += abs(st) * (cnt - 1)
        return (t.name, plo, phi, flo * dsz, (flo + span) * dsz)
    span = 1
    for st, cnt in aps:
        span += abs(st) * (cnt - 1)
    return (t.name, 0, 1, off * dsz, (off + span) * dsz)


class Rec:
    __slots__ = ("plo", "phi", "lo", "hi", "writer", "readers")

    def __init__(self, plo, phi, lo, hi, writer):
        self.plo, self.phi, self.lo, self.hi = plo, phi, lo, hi
        self.writer = writer
        self.readers = []


class Fw:
    ENGS = ("pe", "act", "dve", "pool", "sp")

    def __init__(self, nc):
        self.nc = nc
        self.q = {e: [] for e in self.ENGS}
        self.regions = {}
        self.untracked = set()
        self.dma_cnt = {}
        self.psum_last = {}

    def _track(self, ap, op, is_write):
        name, plo, phi, lo, hi = _region(ap)
        if name in self.untracked:
            return
        if str(ap.space) == "PSUM":
            for b in range(lo // 2048, (hi - 1) // 2048 + 1):
                d = self.psum_last.setdefault(b, {})
                for e2, o2 in d.items():
                    if e2 != op.eng:
                        op.deps.add(o2)
                d[op.eng] = op
        recs = self.regions.setdefault(name, [])
        deps = op.deps
        if is_write:
            keep = []
            for r in recs:
                if r.plo < phi and plo < r.phi and r.lo < hi and lo < r.hi:
                    if r.writer is not None:
                        deps.add(r.writer)
                    deps.update(r.readers)
                    if plo <= r.plo and r.phi <= phi and lo <= r.lo and r.hi <= hi:
                        continue
                    keep.append(r)
                else:
                    keep.append(r)
            keep.append(Rec(plo, phi, lo, hi, op))
            self.regions[name] = keep
        else:
            cover = False
            for r in recs:
                if r.plo < phi and plo < r.phi and r.lo < hi and lo < r.hi:
                    if r.writer is not None:
                        deps.add(r.writer)
                    r.readers.append(op)
                    if r.plo <= plo and phi <= r.phi and r.lo <= lo and hi <= r.hi:
                        cover = True
            if not cover:
                r = Rec(plo, phi, lo, hi, None)
                r.readers.append(op)
                recs.append(r)

    def op(self, eng, fn, reads=(), writes=(), dma_key=None):
        o = Op(eng, fn, dma_key)
        for ap in reads:
            if ap is not None and not isinstance(ap, (int, float)):
                self._track(ap, o, False)
        for ap in writes:
            self._track(ap, o, True)
        o.deps.discard(o)
        if dma_key is not None:
            c = self.dma_cnt.get(dma_key, 0) + 16
            self.dma_cnt[dma_key] = c
            o.dma_val = c
        self.q[eng].append(o)
        return o

    def mm(self, out, lhsT, rhs, start=True, stop=True):
        return self.op("pe", lambda e: e.matmul(out, lhsT, rhs, start=start, stop=stop),
                       reads=[lhsT, rhs], writes=[out])

    def tr(self, out, in_, ident):
        return self.op("pe", lambda e: e.transpose(out, in_, ident), reads=[in_, ident], writes=[out])

    def act(self, out, in_, func, bias=None, scale=None, accum_out=None):
        kw = {}
        rd = [in_]
        if bias is not None:
            kw["bias"] = bias
            rd.append(bias)
        if scale is not None:
            kw["scale"] = scale
            rd.append(scale)
        wr = [out]
        if accum_out is not None:
            kw["accum_out"] = accum_out
            wr.append(accum_out)
        return self.op("act", lambda e: e.activation(out, in_, func, **kw), reads=rd, writes=wr)

    def tt(self, eng, out, in0, in1, op):
        return self.op(eng, lambda e: e.tensor_tensor(out, in0, in1, op), reads=[in0, in1], writes=[out])

    def ts(self, eng, out, in0, s1, s2=None, op0=ALU.mult, op1=None):
        kw = {}
        if op1 is not None:
            kw["op1"] = op1
        return self.op(eng, lambda e: e.tensor_scalar(out, in0, s1, s2, op0, **kw),
                       reads=[in0, s1, s2], writes=[out])

    def stt(self, eng, out, in0, scalar, in1, op0, op1):
        return self.op(eng, lambda e: e.scalar_tensor_tensor(out, in0, scalar, in1, op0, op1),
                       reads=[in0, scalar, in1], writes=[out])

    def copy(self, eng, out, in_):
        if eng == "act":
            return self.op("act", lambda e: e.copy(out, in_), reads=[in_], writes=[out])
        return self.op(eng, lambda e: e.tensor_copy(out, in_), reads=[in_], writes=[out])

    def memset(self, eng, ap, val):
        return self.op(eng, lambda e: e.memset(ap, val), writes=[ap])

    def recip(self, out, in_):
        return self.op("dve", lambda e: e.reciprocal(out, in_), reads=[in_], writes=[out])

    def reduce(self, eng, out, in_, op=ALU.add, axis=AX.X):
        return self.op(eng, lambda e: e.tensor_reduce(out, in_, axis, op), reads=[in_], writes=[out])

    def dma(self, queue, out, in_, key):
        return self.op(queue, lambda e: e.dma_start(out, in_), reads=[in_], writes=[out], dma_key=key)

    def emit(self):
        nc = self.nc
        for e in self.ENGS:
            for o in self.q[e]:
                for d in o.deps:
                    if d.dma_key is None and not (d.eng == "pe" and o.eng == "pe"):
                        d.milestone = True
        for e in self.ENGS:
            n = 0
            for o in self.q[e]:
                if o.milestone:
                    n += 1
                    o.ms_no = n
        with contextlib.ExitStack() as st:
            esem = {e: st.enter_context(nc.semaphore("s_" + e)) for e in self.ENGS}
            dsem = {k: st.enter_context(nc.semaphore("d_%s" % (k,))) for k in self.dma_cnt}
            block = st.enter_context(nc.Block())
            fw = self

            def run(engname, eng):
                waited = {}
                for o in fw.q[engname]:
                    need = {}
                    for d in o.deps:
                        if d.dma_key is not None:
                            s, v = ("d", d.dma_key), d.dma_val
                        else:
                            if d.eng == "pe" and engname == "pe":
                                continue
                            s, v = ("e", d.eng), d.ms_no
                        if waited.get(s, 0) >= v:
                            continue
                        if need.get(s, 0) < v:
                            need[s] = v
                    for s, v in need.items():
                        eng.wait_ge(dsem[s[1]] if s[0] == "d" else esem[s[1]], v)
                        waited[s] = v
                    ins = o.fn(eng)
                    if o.dma_key is not None:
                        ins.then_inc(dsem[o.dma_key], 16)
                    elif o.milestone:
                        ins.then_inc(esem[engname], 1)
                if engname == "sp":
                    for k, v in fw.dma_cnt.items():
                        if waited.get(("d", k), 0) < v:
                            eng.wait_ge(dsem[k], v)

            @block.tensor
            def _(eng):
                run("pe", eng)

            @block.scalar
            def _(eng):
                run("act", eng)

            @block.vector
            def _(eng):
                run("dve", eng)

            @block.gpsimd
            def _(eng):
                run("pool", eng)

            @block.sync
            def _(eng):
                run("sp", eng)


class Arena:
    def __init__(self, t, words, base=0):
        self.t = t
        self.words = words
        self.base = base
        self.top = 0
        self.peak = 0

    def alloc(self, shape, dt=F32):
        shape = [int(s) for s in (shape if isinstance(shape, (list, tuple)) else [shape])]
        n = int(np.prod(shape))
        nbytes = n * (2 if dt == BF16 else 4)
        w = (nbytes + 31) // 32 * 8
        off = self.base + self.top
        self.top += w
        self.peak = max(self.peak, self.top)
        assert self.top <= self.words, "arena overflow: %d > %d words" % (self.top, self.words)
        v = self.t[:, off:off + (nbytes + 3) // 4]
        if dt == BF16:
            v = v.bitcast(BF16)
        if len(shape) > 1:
            names = ["a%d" % i for i in range(len(shape))]
            pat = "p (%s) -> p %s" % (" ".join(names), " ".join(names))
            v = v.rearrange(pat, **{names[i]: shape[i] for i in range(1, len(shape))})
        return v

    def mark(self):
        return self.top

    def release(self, m):
        self.top = m


class WStream:
    def __init__(self, f, slots, name, queue="pool", depth=1):
        self.f, self.slots, self.name, self.queue, self.depth = f, slots, name, queue, depth
        self.loads = []
        self.issued = 0
        self.next = 0

    def plan(self, src, view=None):
        self.loads.append((src, view))

    def ahead(self, n=1):
        tgt = min(self.next - 1 + n, len(self.loads) - 1)
        ns = len(self.slots)
        while self.issued <= tgt:
            j = self.issued
            src, view = self.loads[j]
            slot = self.slots[j % ns]
            self.f.dma(self.queue, view(slot) if view else slot, src, "%s%d" % (self.name, j % ns))
            self.issued += 1

    def get(self, depth=None):
        i = self.next
        self.next += 1
        n = len(self.slots)
        depth = self.depth if depth is None else depth
        while self.issued <= min(i + depth, len(self.loads) - 1):
            j = self.issued
            src, view = self.loads[j]
            slot = self.slots[j % n]
            self.f.dma(self.queue, view(slot) if view else slot, src, "%s%d" % (self.name, j % n))
            self.issued += 1
        src, view = self.loads[i]
        slot = self.slots[i % n]
        return view(slot) if view else slot


D = 2048
KC = 16
T = 1024
NT = 8
H = 8
P_IN = 6176
DFF = 5632
NJ = 44
NG = 4
JG = NJ // NG
SV0, AB0, HD0, U0 = 0, 1024, 1056, 5152
EPS = 1e-6
NEG = -30000.0

C_ID, C_ONE = 0, 128
NCST0 = 256
C_TRIF, C_TRIB, C_BLK, C_CH0, C_CH1, C_NMF, C_NMB = 0, 128, 256, 384, 512, 640, 896
NCST1 = 1152
PR_C, PR_N1, PR_N2, PR_FN, PR_GN, PR_CA, PR_FW, PR_FB = 0, 32, 48, 64, 80, 81, 201, 993
NPRM = 1081
RW_DTB, RW_ALOG, RW_SNW, RW_SGB = 0, 16, 32, 1056
NROW = 2080


def _make_consts():
    c0 = np.zeros((128, NCST0), np.float32)
    c0[:, C_ID:C_ID + 128] = np.eye(128)
    c0[:, C_ONE:C_ONE + 128] = 1.0
    c = np.zeros((128, NCST1), np.float32)
    idx = np.arange(128)
    same = (idx[:, None] // 64) == (idx[None, :] // 64)
    c[:, C_TRIF:C_TRIF + 128] = (same & (idx[:, None] <= idx[None, :]))
    c[:, C_TRIB:C_TRIB + 128] = (same & (idx[:, None] >= idx[None, :]))
    c[:, C_BLK:C_BLK + 128] = same
    c[:, C_CH0:C_CH0 + 128] = (idx[:, None] < 64)
    c[:, C_CH1:C_CH1 + 128] = (idx[:, None] >= 64)
    nmf = np.full((128, 2, 128), NEG, np.float32)
    nmf[:, 0][same & (idx[None, :] >= idx[:, None])] = 0.0
    nmf[:, 1][same & (idx[None, :] > idx[:, None])] = 0.0
    nmb = np.full((128, 2, 128), NEG, np.float32)
    nmb[:, 0][same & (idx[None, :] <= idx[:, None])] = 0.0
    nmb[:, 1][same & (idx[None, :] < idx[:, None])] = 0.0
    c[:, C_NMF:C_NMF + 256] = nmf.reshape(128, 256)
    c[:, C_NMB:C_NMB + 256] = nmb.reshape(128, 256)
    return c0, c


def build_program(stop_after=None, dumps=(), units=(0, 1)):
    nc = bass.Bass("TRN2", target_bir_lowering=False)
    din = lambda n, s: nc.dram_tensor(n, list(s), F32, kind="ExternalInput").ap()
    dout = lambda n, s: nc.dram_tensor(n, list(s), F32, kind="ExternalOutput").ap()
    xT_d = din("xT", [2, D, T])
    cst0_d = din("cst0", [128, NCST0])
    cst1_d = din("cst1", [128, NCST1])
    prm_d = din("prm", [128, NPRM])
    row_d = din("rowp", [1, NROW])
    s0_d = din("s0", [128, 2, H, 128])
    sgw_d = din("sgw", [128, 8, 128])
    w_ada_d = din("w_ada", [D, 6 * D])
    b_ada_d = din("b_ada", [1, 6 * D])
    w_in_d = din("w_in", [D, P_IN])
    w_out_d = din("w_out", [D, D])
    w_up_d = din("w_up", [D, 2 * DFF])
    w_dn_d = din("w_down", [DFF, D])
    y_d = dout("y", [2, T, D])
    nsf_d = dout("nsf", [4, H, 128, 128])
    nsb_d = dout("nsb", [4, H, 128, 128])
    dump_d = {n: dout("dbg_" + n, s) for n, s in dumps}

    f = Fw(nc)
    f.untracked |= {"xT", "cst0", "cst1", "prm", "rowp", "s0", "sgw", "w_ada", "b_ada", "w_in", "w_out", "w_up", "w_down"}
    dbgctr = [0]

    with contextlib.ExitStack() as st:
        ARW = 52000
        arena_t = st.enter_context(nc.sbuf_tensor("arena", [128, ARW], F32))
        ps = st.enter_context(nc.psum_tensor("ps", [128, 8, 512], F32))

        def bank(b, n=1):
            if n == 1:
                return ps[:, b, :]
            return ps[:, b:b + n, :].rearrange("p a b -> p (a b)")

        W_HT, W_OT, W_XR, W_SC = 8192, 8192, 16384, 8704
        W_P = ARW - (W_HT + W_OT + W_XR + W_SC)
        AP_ = Arena(arena_t, W_P, 0)
        o_ht = W_P
        o_ot = o_ht + W_HT
        o_xr = o_ot + W_OT
        o_sc = o_xr + W_XR
        hT = arena_t[:, o_ht:o_ht + W_HT].bitcast(BF16).rearrange("p (c t) -> p c t", c=16)
        oT = arena_t[:, o_ot:o_ot + W_OT].bitcast(BF16).rearrange("p (c t) -> p c t", c=16)
        XR = arena_t[:, o_xr:o_xr + W_XR].rearrange("p (c t) -> p c t", c=16)
        A_O = Arena(arena_t, W_OT, o_ot)
        A_B = Arena(arena_t, W_XR + W_SC, o_xr)
        A_S = Arena(arena_t, W_SC, o_sc)

        cst0 = AP_.alloc([NCST0])
        prm = AP_.alloc([NPRM])
        identb = AP_.alloc([128], BF16)
        onesb = AP_.alloc([128], BF16)
        epsT = AP_.alloc([1])
        MODC = AP_.alloc([2, 6, 16])
        GCOL = AP_.alloc([2, 2, 16])
        f.dma("sp", cst0, cst0_d, "cst0")
        f.dma("sp", prm, prm_d, "prm")
        f.memset("dve", epsT, EPS)
        ident = cst0[:, C_ID:C_ID + 128]
        ones = cst0[:, C_ONE:C_ONE + 128]
        f.copy("dve", identb, ident)
        f.copy("dve", onesb, ones)

        def dump(name, ap_sb):
            if name in dump_d:
                dbgctr[0] += 1
                f.dma("pool" if ap_sb.dtype == BF16 else "sp", dump_d[name], ap_sb, "dbg%d" % dbgctr[0])

        NWS = 2
        wslot = [AP_.alloc([16, 512], BF16) for _ in range(NWS)]
        WS = WStream(f, wslot, "ws")
        wsrc = lambda ap: ap.rearrange("(c p) n -> p c n", p=128)
        for blk in range(24):
            WS.plan(wsrc(w_ada_d[:, blk * 512:(blk + 1) * 512]))
        for u_ in units:
            WS.plan(wsrc(w_in_d[:, SV0:SV0 + 512]))
            WS.plan(wsrc(w_in_d[:, SV0 + 512:SV0 + 1024]))
            for g_ in range(2):
                WS.plan(wsrc(w_in_d[:, U0 + g_ * 512:U0 + (g_ + 1) * 512]))
            for h_ in range(H):
                WS.plan(wsrc(w_in_d[:, HD0 + h_ * 512:HD0 + (h_ + 1) * 512]))
            for c_ in range(4):
                WS.plan(wsrc(w_out_d[:, c_ * 512:(c_ + 1) * 512]))
            for j_ in range(NJ // 2):
                WS.plan(wsrc(w_up_d[:, j_ * 512:(j_ + 1) * 512]))
        WD = None

        def phase_A():
            A_B.release(0)
            sc = A_B.alloc([2, 16])
            sbc = A_B.alloc([2, 16, 128], BF16)
            f.act(sc, prm[:, PR_C:PR_C + 32].rearrange("p (u c) -> p u c", u=2), AF.Silu)
            f.copy("dve", sbc, sc.unsqueeze(3).broadcast_to([128, 2, 16, 128]))
            bb = [A_B.alloc([2048]) for _ in range(2)]
            tmpm = [A_B.alloc([2048]) for _ in range(2)]
            scr = A_B.alloc([16, 128])
            for s in range(6):
                f.dma("sp", bb[s % 2], b_ada_d[:, s * D:(s + 1) * D].partition_broadcast(128), "bb%d" % (s % 2))
                for cb in range(4):
                    blk = s * 4 + cb
                    wa = WS.get()
                    for u in range(2):
                        pb = bank((blk * 2 + u) % 8)
                        for kk in range(KC):
                            f.mm(pb, sbc[:, u, kk, :], wa[:, kk, :], start=(kk == 0), stop=(kk == KC - 1))
                        f.tt("dve", tmpm[u][:, cb * 512:(cb + 1) * 512], pb, bb[s % 2][:, cb * 512:(cb + 1) * 512], ALU.add)
                for u in range(2):
                    f.tt("dve", scr, tmpm[u].rearrange("p (c j) -> p c j", c=16),
                         ident.unsqueeze(1).broadcast_to([128, 16, 128]), ALU.mult)
                    f.reduce("dve", MODC[:, u, s, :], scr)
            for u in range(2):
                f.stt("dve", GCOL[:, u, 0, :], MODC[:, u, 1, :], 1.0, prm[:, PR_N1:PR_N1 + 16], ALU.add, ALU.mult)
                f.stt("dve", GCOL[:, u, 1, :], MODC[:, u, 4, :], 1.0, prm[:, PR_N2:PR_N2 + 16], ALU.add, ALU.mult)

        def rstd_rows(src, AR, load_fn=None):
            rr = AR.alloc([1024])
            sq = [AR.alloc([1024], BF16) for _ in range(2)]
            for c in range(KC):
                if load_fn is not None:
                    load_fn(c)
                f.act(sq[c % 2], src[:, c, :], AF.Square)
                for half in range(2):
                    f.mm(bank(half), onesb, sq[c % 2][:, half * 512:(half + 1) * 512], start=(c == 0), stop=(c == KC - 1))
            f.act(rr, bank(0, 2), AF.Ln, bias=epsT, scale=1.0 / D)
            f.act(rr, rr, AF.Exp, scale=-0.5)
            return rr

        def modulate_to(u, which, src, rr, AR, dst):
            tmp = [AR.alloc([1024]) for _ in range(2)]
            sec_shift = 0 if which == 0 else 3
            for c in range(KC):
                t = tmp[c % 2]
                f.stt("dve", t, src[:, c, :], GCOL[:, u, which, c:c + 1], rr, ALU.mult, ALU.mult)
                f.act(dst[:, c, :], t, AF.Identity, bias=MODC[:, u, sec_shift, c:c + 1])

        def unit(u):
            nseq, L = (4, 256) if u == 0 else (1, 1024)

            A_S.release(0)

            def load_x(c):
                f.dma("sp", XR[:, c, :], xT_d[u, c * 128:(c + 1) * 128, :], "xT%d" % c)
            rr = rstd_rows(XR, A_S, load_x)
            modulate_to(u, 0, XR, rr, A_S, hT)
            if u == 0:
                dump("hT", hT[:, :, 0:256])
            if stop_after == "B":
                return

            A_B.release(0)
            cst1 = A_B.alloc([NCST1])
            f.dma("sp", cst1, cst1_d, "cst1")
            GA = A_B.alloc([8, 16, 2])
            BET = A_B.alloc([8, 16])
            BEG = A_B.alloc([8, 16])
            EK = A_B.alloc([8, 16])
            GLB = A_B.alloc([8, 2, 16])
            mC1 = A_B.mark()
            svn = A_B.alloc([8, 1024], BF16)
            rowb = A_B.alloc([32])
            snw_b = A_B.alloc([1024])
            f.dma("sp", rowb, row_d[:, 0:32].partition_broadcast(128), "rwa")
            f.dma("sp", snw_b, row_d[:, RW_SNW:RW_SNW + 1024].partition_broadcast(128), "rwb")
            negA = A_B.alloc([16])
            f.act(negA, rowb[:, 16:32], AF.Exp)
            f.ts("dve", negA, negA, -1.0, None, op0=ALU.mult)
            wab = A_B.alloc([16, 32], BF16)
            f.dma("pool", wab, w_in_d[:, AB0:AB0 + 32].rearrange("(c p) n -> p c n", p=128), "wab")
            ws0 = WS.get(depth=1)
            ws1 = WS.get(depth=0)
            ABR = A_B.alloc([8, 32])
            gsv = [A_B.alloc([1024]) for _ in range(2)]
            junk = A_B.alloc([1024], BF16)
            ss2 = A_B.alloc([8])
            rs2 = A_B.alloc([8])
            for m in range(NT):
                b0 = (m % 2) * 3
                for kk in range(KC):
                    lt = hT[:, kk, m * 128:(m + 1) * 128]
                    f.mm(bank(b0), lt, ws0[:, kk, :], start=(kk == 0), stop=(kk == KC - 1))
                    f.mm(bank(b0 + 1), lt, ws1[:, kk, :], start=(kk == 0), stop=(kk == KC - 1))
                    f.mm(bank(b0 + 2)[:, 0:32], lt, wab[:, kk, :], start=(kk == 0), stop=(kk == KC - 1))
                g = gsv[m % 2]
                f.act(g, bank(b0, 2), AF.Gelu_apprx_tanh)
                f.act(junk, g, AF.Square, accum_out=ss2[:, m:m + 1])
                f.copy("dve", ABR[:, m, :], bank(b0 + 2)[:, 0:32])
                f.act(rs2[:, m:m + 1], ss2[:, m:m + 1], AF.Sqrt, bias=epsT, scale=1.0 / 1024)
                f.recip(rs2[:, m:m + 1], rs2[:, m:m + 1])
                f.stt("dve", svn[:, m, :], g, rs2[:, m:m + 1], snw_b, ALU.mult, ALU.mult)
            WS.ahead(1)
            zt = A_B.alloc([8, 16])
            GL = A_B.alloc([8, 16])
            LNB = A_B.alloc([8, 16])
            EGt = A_B.alloc([8, 16])
            f.tt("dve", zt, ABR[:, :, 0:16], rowb[:, 0:16].unsqueeze(1).broadcast_to([128, 8, 16]), ALU.add)
            f.act(zt, zt, AF.Exp)
            f.act(zt, zt, AF.Ln, bias=1.0)
            f.tt("dve", GL, zt, negA.unsqueeze(1).broadcast_to([128, 8, 16]), ALU.mult)
            f.act(BET, ABR[:, :, 16:32], AF.Sigmoid)
            f.act(LNB, BET, AF.Ln)
            pc = bank(7).rearrange("p (m c) -> p m c", m=8)
            for m in range(NT):
                f.mm(pc[:, m, 0:8], cst1[:, C_TRIF:C_TRIF + 128], GL[:, m, 0:8])
                f.mm(pc[:, m, 8:16], cst1[:, C_TRIB:C_TRIB + 128], GL[:, m, 8:16])
                f.mm(pc[:, m, 16:32], cst1[:, C_BLK:C_BLK + 128], GL[:, m, :])
                f.mm(pc[:, m, 32:48], cst1[:, C_CH0:C_CH0 + 128], GL[:, m, :])
                f.mm(pc[:, m, 48:64], cst1[:, C_CH1:C_CH1 + 128], GL[:, m, :])
            f.copy("dve", GA[:, :, :, 0], pc[:, :, 0:16])
            f.tt("dve", GA[:, :, :, 1], pc[:, :, 0:16], LNB, ALU.add)
            f.act(EGt, pc[:, :, 0:16], AF.Exp)
            f.tt("dve", BEG, EGt, BET, ALU.mult)
            f.tt("dve", zt, pc[:, :, 16:32], GA[:, :, :, 0], ALU.subtract)
            f.act(EK, zt, AF.Exp)
            f.act(GLB, pc[:, :, 32:64].rearrange("p m (a c) -> p m a c", a=2), AF.Exp)
            if u == 0:
                dump("svn", svn[:, 0:2, :])
                dump("GL", GL)
                dump("BET", BET)
            if stop_after == "C1":
                return

            sgwT = A_B.alloc([8, 128], BF16)
            f.dma("pool", sgwT, sgw_d, "sgw")
            sgb = A_B.alloc([8, 128])
            f.dma("sp", sgb, row_d[:, RW_SGB:RW_SGB + 1024].partition_broadcast(128), "rwc")
            ug = [A_B.alloc([1024]) for _ in range(2)]
            tmpg = A_B.alloc([1024])
            wu = None
            for g in range(8):
                if g % 4 == 0:
                    wu = WS.get()
                bu = (g % 2) * 4
                for half in range(2):
                    for kk in range(KC):
                        f.mm(bank(bu + half), wu[:, kk, (g % 4) * 128:(g % 4 + 1) * 128],
                             hT[:, kk, half * 512:(half + 1) * 512], start=(kk == 0), stop=(kk == KC - 1))
                f.act(ug[g % 2], bank(bu, 2), AF.Gelu_apprx_tanh)
                pm = bank(bu + 2, 2)
                for n in range(NT):
                    f.mm(pm[:, n * 128:(n + 1) * 128], svn[:, n, g * 128:(g + 1) * 128], sgwT[:, g, :])
                f.tt("dve", tmpg.rearrange("p (n i) -> p n i", n=8), pm.rearrange("p (n i) -> p n i", n=8),
                     sgb[:, g, :].unsqueeze(1).broadcast_to([128, 8, 128]), ALU.add)
                f.tt("pool", oT[:, 8 + g, :], tmpg, ug[g % 2], ALU.mult)
            if u == 0:
                dump("oTb", oT[:, 8:16, 0:256])
            A_B.release(mC1)
            if stop_after == "SGU":
                return

            heads(u, cst1, GA, BET, BEG, EK, GLB, nseq, L)
            if u == 0:
                dump("oTa", oT[:, 0:8, 0:256])
            if stop_after == "HEADS":
                return

            A_S.release(0)
            xq = [A_S.alloc([512]) for _ in range(3)]
            i = 0
            wo = None
            for c in range(KC):
                if c % 4 == 0:
                    wo = WS.get()
                for half in range(2):
                    xt = xq[i % 3]
                    f.dma("sp", xt, xT_d[u, c * 128:(c + 1) * 128, half * 512:(half + 1) * 512], "xq%d" % (i % 3))
                    pb = bank(4 + i % 4)
                    for kk in range(KC):
                        f.mm(pb, wo[:, kk, (c % 4) * 128:(c % 4 + 1) * 128], oT[:, kk, half * 512:(half + 1) * 512],
                             start=(kk == 0), stop=(kk == KC - 1))
                    f.stt("dve", XR[:, c, half * 512:(half + 1) * 512], pb, MODC[:, u, 2, c:c + 1], xt, ALU.mult, ALU.add)
                    i += 1
            if u == 0:
                dump("x1T", XR[:, :, 0:256])
            if stop_after == "D":
                return
            A_S.release(0)
            rr = rstd_rows(XR, A_S)
            modulate_to(u, 1, XR, rr, A_S, hT)
            h2T = hT
            if stop_after == "D2":
                return

            A_O.release(0)
            A_S.release(0)
            actT = A_O.alloc([JG, 1024], BF16)
            PADW = 4 * 258 if u == 0 else 18 * 66
            raws = [A_O.alloc([PADW]) for _ in range(2)]
            for r in raws:
                f.memset("pool", r, 0.0)
            accs2 = [[A_S.alloc([1024]) for _ in range(2)] for _ in range(2)]
            wds = [A_S.alloc([JG, 256], BF16) for _ in range(2)]
            WDs = WStream(f, wds, "wd")
            for grp_ in range(NG):
                for cb_ in range(8):
                    WDs.plan(w_dn_d[grp_ * JG * 128:(grp_ + 1) * JG * 128, cb_ * 256:(cb_ + 1) * 256]
                             .rearrange("(c p) n -> p c n", p=128))
            fw_ = prm[:, PR_FW:PR_FW + 792].rearrange("p (c t) -> p c t", c=88)
            fb_ = prm[:, PR_FB:PR_FB + 88]

            def views(raw, acc):
                if u == 0:
                    return raw.rearrange("p (s l) -> p s l", s=4), acc.rearrange("p (s l) -> p s l", s=4)
                return raw.rearrange("p (r c) -> p r c", r=18), acc.rearrange("p (r c) -> p r c", r=16)

            def win(rv, a, b):
                if u == 0:
                    return rv[:, :, b:b + 256]
                return rv[:, a:a + 16, b:b + 64]

            def inner(rv):
                if u == 0:
                    return rv[:, :, 1:257]
                return rv[:, 1:17, 1:65]

            taps = [(1, 0), (1, 1), (1, 2)] if u == 0 else [(a, b) for a in range(3) for b in range(3)]
            upctr = 0
            dctr = 0
            ectr = 0
            wu_ = None
            def emit_down(grp):
                nonlocal ectr
                for cb in range(8):
                    wd = WDs.get()
                    for dc in range(2):
                        c = cb * 2 + dc
                        for half in range(2):
                            pb = bank(6 + (ectr % 2))
                            ectr += 1
                            for jj_ in range(JG):
                                f.mm(pb, wd[:, jj_, dc * 128:(dc + 1) * 128], actT[:, jj_, half * 512:(half + 1) * 512],
                                     start=(jj_ == 0), stop=(jj_ == JG - 1))
                            xs = XR[:, c, half * 512:(half + 1) * 512]
                            f.stt("dve", xs, pb, MODC[:, u, 5, c:c + 1], xs, ALU.mult, ALU.add)

            deferred = []
            for j in range(NJ):
                grp, jj = divmod(j, JG)
                if j % 2 == 0:
                    wu_ = WS.get()
                accs = accs2[j % 2]
                for t2 in range(2):
                    cj = 2 * j + t2
                    col0 = (j % 2) * 256 + t2 * 128
                    b0 = (upctr % 3) * 2
                    upctr += 1
                    for half in range(2):
                        for kk in range(KC):
                            f.mm(bank(b0 + half), wu_[:, kk, col0:col0 + 128],
                                 h2T[:, kk, half * 512:(half + 1) * 512], start=(kk == 0), stop=(kk == KC - 1))
                    rv, av = views(raws[t2], accs[t2])
                    src = bank(b0, 2)
                    if u == 0:
                        src = src.rearrange("p (s l) -> p s l", s=4)
                    else:
                        src = src.rearrange("p (r c) -> p r c", r=16)
                    f.copy("act", inner(rv), src)
                    for ti, (a, b) in enumerate(taps):
                        wsc = fw_[:, cj, a * 3 + b:a * 3 + b + 1]
                        if ti == 0:
                            f.act(av, win(rv, a, b), AF.Identity, scale=wsc, bias=fb_[:, cj:cj + 1])
                        else:
                            f.stt("dve", av, win(rv, a, b), wsc, av, ALU.mult, ALU.add)
                f.act(accs[0], accs[0], AF.Silu)
                mult = (lambda jj_, a_: (lambda: f.tt("dve" if u == 1 else "pool", actT[:, jj_, :], a_[0], a_[1], ALU.mult)))(jj, accs)
                if grp > 0 and jj < 2:
                    deferred.append(mult)
                    if jj == 1:
                        emit_down(grp - 1)
                        for m_ in deferred:
                            m_()
                        deferred = []
                else:
                    mult()
                if u == 0 and j == 0:
                    dump("act0", actT[:, 0, 0:256])
            emit_down(NG - 1)
            if u == 0:
                dump("x2T", XR[:, :, 0:256])

            A_O.release(0)
            A_S.release(0)
            rr = rstd_rows(XR, A_S)
            for c in range(KC):
                f.stt("dve", XR[:, c, :], XR[:, c, :], prm[:, PR_FN + c:PR_FN + c + 1], rr, ALU.mult, ALU.mult)
            ys = [A_O.alloc([2048]) for _ in range(2)]
            for m in range(NT):
                pb = bank(4 * (m % 2), 4)
                for c in range(KC):
                    f.tr(pb[:, c * 128:(c + 1) * 128], XR[:, c, m * 128:(m + 1) * 128], ident)
                f.copy("act", ys[m % 2][:, 0:1024], pb[:, 0:1024])
                f.copy("dve", ys[m % 2][:, 1024:2048], pb[:, 1024:2048])
                f.dma("sp", y_d[u, m * 128:(m + 1) * 128, :], ys[m % 2], "ys%d" % (m % 2))

        def heads(u, cst1, GA, BET, BEG, EK, GLB, nseq, L):
            ca = prm[:, PR_CA:PR_CA + 120].rearrange("p (h t k) -> p h t k", h=8, t=3)
            gnw = prm[:, PR_GN:PR_GN + 1]
            QKV = [[A_B.alloc([1024], BF16), A_B.alloc([1024]), A_B.alloc([1024]), A_B.alloc([1024], BF16)] for _ in range(2)]
            PADW = nseq * (L + 4)
            raw = A_B.alloc([PADW])
            f.memset("pool", raw, 0.0)
            rawv = raw.rearrange("p (s l) -> p s l", s=nseq)
            acc = A_B.alloc([1024])
            accv = acc.rearrange("p (s l) -> p s l", s=nseq)
            rn = A_B.alloc([1024])
            osums = [A_B.alloc([1024]) for _ in range(2)]
            rnf = A_B.alloc([1024])
            zeroS = A_B.alloc([128])
            f.memset("pool", zeroS, 0.0)
            Sst = [[A_B.alloc([128]) for _ in range(2)] for _ in range(2)]
            Sbf = [[A_B.alloc([128], BF16) for _ in range(2)] for _ in range(2)]
            sout = [A_B.alloc([128]) for _ in range(2)]
            NM = (cst1[:, C_NMF:C_NMF + 256].rearrange("p (a b) -> p a b", a=2),
                  cst1[:, C_NMB:C_NMB + 256].rearrange("p (a b) -> p a b", a=2))

            def prep_bufs():
                return dict(E=A_B.alloc([2, 128]), R0=A_B.alloc([256]), W=A_B.alloc([3, 128]), eg=A_B.alloc([128]))

            def scan_bufs():
                return dict(AT=A_B.alloc([128], BF16), qd=A_B.alloc([128], BF16), kd=A_B.alloc([128], BF16),
                            u=A_B.alloc([128]), wT=A_B.alloc([128], BF16), vn=A_B.alloc([128], BF16))
            PB = [prep_bufs() for _ in range(4)]
            SB = [[scan_bufs() for _ in range(4)] for _ in range(2)]

            def proj_gen(h, dst):
                wh = WS.get()
                yield
                for t in range(4):
                    for half in range(2):
                        for kk in range(KC):
                            f.mm(bank(half), wh[:, kk, t * 128:(t + 1) * 128], hT[:, kk, half * 512:(half + 1) * 512],
                                 start=(kk == 0), stop=(kk == KC - 1))
                            if kk % 4 == 3:
                                yield
                    src = bank(0, 2)
                    if t == 3:
                        f.act(oT[:, h, :], src, AF.Silu)
                        yield
                        continue
                    f.copy("act", rawv[:, :, 2:2 + L], src.rearrange("p (s l) -> p s l", s=nseq))
                    yield
                    for tap in range(5):
                        wsc = ca[:, h, t, tap:tap + 1]
                        if tap == 0:
                            f.ts("dve", accv, rawv[:, :, 0:L], wsc, None, op0=ALU.mult)
                        else:
                            f.stt("dve", accv, rawv[:, :, tap:tap + L], wsc, accv, ALU.mult, ALU.add)
                        yield
                    if t == 2:
                        f.act(dst[2], acc, AF.Silu)
                        yield
                        continue
                    f.act(acc, acc, AF.Silu)
                    yield
                    f.act(rn, acc, AF.Square)
                    yield
                    for half in range(2):
                        f.mm(bank(half), ones, rn[:, half * 512:(half + 1) * 512])
                        yield
                        f.act(rn[:, half * 512:(half + 1) * 512], bank(half), AF.Ln, bias=epsT)
                        yield
                    f.act(rn, rn, AF.Exp, scale=-0.5)
                    yield
                    if t == 0:
                        f.stt("dve", dst[0], acc, float(128 ** -0.5), rn, ALU.mult, ALU.mult)
                    else:
                        f.tt("dve", dst[1], acc, rn, ALU.mult)
                        yield
                        f.copy("act", dst[3], dst[1])
                    yield

            def prep(h, m, d, qkv, B, S_, ci):
                cd = d * 8 + h
                qT = qkv[0][:, m * 128:(m + 1) * 128]
                kT = qkv[1][:, m * 128:(m + 1) * 128]
                vT = qkv[2][:, m * 128:(m + 1) * 128]
                kTb = qkv[3][:, m * 128:(m + 1) * 128]
                bk = bank(4 + ci)
                slot = [bk[:, i * 128:(i + 1) * 128] for i in range(4)]
                kqkk = bk[:, 0:256].rearrange("p (a b) -> p a b", a=2)
                ktok, vtok = slot[2], slot[3]
                f.mm(kqkk[:, 0, :], kTb, qT)
                f.mm(kqkk[:, 1, :], kTb, kTb)
                f.tr(ktok, kT, ident)
                f.tr(vtok, vT, ident)
                yield
                gc = GA[:, m, cd, 0:1]
                E = B["E"]
                f.tt("dve", E, ident.unsqueeze(1).broadcast_to([128, 2, 128]),
                     GA[:, m, cd, :].unsqueeze(2).broadcast_to([128, 2, 128]), ALU.mult)
                f.act(B["R0"][:, 0:128], vtok, AF.Identity, scale=BET[:, m, cd:cd + 1])
                yield
                f.act(B["R0"][:, 128:256], ktok, AF.Identity, scale=BEG[:, m, cd:cd + 1])
                f.ts("dve", S_["kd"], ktok, EK[:, m, cd:cd + 1], None, op0=ALU.mult)
                yield
                rows = bk[:, 256:512]
                f.mm(rows, ones, E.rearrange("p a b -> p (a b)"))
                yield
                rows3 = rows.rearrange("p (a b) -> p a b", a=2)
                f.stt("dve", E, rows3, gc, NM[d], ALU.subtract, ALU.min)
                f.act(B["eg"], rows3[:, 0, :], AF.Exp)
                yield
                f.act(E, E, AF.Exp)
                f.tt("pool", S_["qd"], B["eg"], qT, ALU.mult)
                yield
                W = B["W"]
                NT_ = W[:, 1, :]
                f.tt("dve", NT_, kqkk[:, 1, :], E[:, 1, :], ALU.mult)
                f.tt("dve", S_["AT"], kqkk[:, 0, :], E[:, 0, :], ALU.mult)
                yield
                f.tr(slot[0], NT_, ident)
                f.tt("pool", W[:, 0, :], ident, NT_, ALU.subtract)
                yield
                f.copy("act", W[:, 2, :], slot[0])
                yield
                f.mm(slot[1], W[:, 2, :], W[:, 1, :])
                f.mm(slot[2], W[:, 1, :], W[:, 2, :])
                yield
                f.copy("act", W[:, 1:3, :], bk[:, 128:384].rearrange("p (a b) -> p a b", a=2))
                yield
                for lev in range(1, 5):
                    f.mm(bk[:, 0:256], W[:, 2, :], W[:, 0:2, :].rearrange("p a b -> p (a b)"))
                    f.mm(slot[2], W[:, 1, :], W[:, 2, :])
                    yield
                    f.tt("dve", W[:, 0, :], W[:, 0, :], slot[0], ALU.add)
                    f.copy("act", W[:, 1:3, :], bk[:, 128:384].rearrange("p (a b) -> p a b", a=2))
                    yield
                f.mm(slot[0], W[:, 2, :], W[:, 0, :])
                yield
                f.tt("dve", W[:, 0, :], W[:, 0, :], slot[0], ALU.add)
                yield
                Tt = W[:, 0, :]
                f.mm(slot[1], Tt, B["R0"][:, 0:128])
                f.mm(slot[2], B["R0"][:, 128:256], Tt)
                yield
                f.copy("act", S_["u"], slot[1])
                f.copy("dve", S_["wT"], slot[2])
                yield

            def init_state(d, h, idx):
                dst = Sst[d][idx]
                if u == 1:
                    f.dma("sp", dst, s0_d[:, d, h, :], "s0%d" % d)
                else:
                    f.copy("pool", dst, zeroS)
                f.copy("act", Sbf[d][idx], dst)

            Sidx = [0, 0]

            def scan_chain(h, d, tiles, Bs):
                cd = d * 8 + h
                bS = bank(2 + d)
                vnp = bS[:, 0:128]
                otp = bS[:, 128:192]
                for m, B in zip(tiles, Bs):
                    for ci in range(2):
                        c = ci if d == 0 else 1 - ci
                        lo, hi = c * 64, (c + 1) * 64
                        if u == 0:
                            first = (m % 2 == 0 and c == 0) if d == 0 else (m % 2 == 1 and c == 1)
                            if first:
                                init_state(d, h, Sidx[d])
                        Scur, Snew = Sst[d][Sidx[d]], Sst[d][1 - Sidx[d]]
                        Sbc, Sbn = Sbf[d][Sidx[d]], Sbf[d][1 - Sidx[d]]
                        f.mm(vnp, B["wT"], Sbc)
                        yield
                        f.tt("dve", B["vn"][lo:hi, :], B["u"][lo:hi, :], vnp[lo:hi, :], ALU.subtract)
                        yield
                        f.mm(otp, Sbc, B["qd"][:, lo:hi], start=True, stop=False)
                        f.mm(otp, B["vn"][lo:hi, :], B["AT"][lo:hi, lo:hi], start=False, stop=True)
                        f.mm(vnp, B["kd"][lo:hi, :], B["vn"][lo:hi, :])
                        yield
                        t0 = m * 128 + lo
                        f.stt("dve", Snew, Scur, GLB[:, m, c, cd:cd + 1], vnp, ALU.mult, ALU.add)
                        osum = osums[h % 2]
                        f.tt("dve", osum[:, t0:t0 + 64], osum[:, t0:t0 + 64], otp, ALU.add)
                        yield
                        f.copy("act", Sbn, Snew)
                        yield
                        Sidx[d] = 1 - Sidx[d]
                        if u == 0:
                            last = (m % 2 == 1 and c == 1) if d == 0 else (m % 2 == 0 and c == 0)
                            if last:
                                seq = m // 2
                                f.copy("pool", sout[d], Sst[d][Sidx[d]])
                                dst = (nsf_d if d == 0 else nsb_d)[seq, h, :, :]
                                f.dma("sp", dst, sout[d], "so%d" % d)

            def finalize(h):
                osum = osums[h % 2]
                f.act(rnf, osum, AF.Square)
                yield
                for half in range(2):
                    f.mm(bank(half), ones, rnf[:, half * 512:(half + 1) * 512])
                    yield
                    f.act(rnf[:, half * 512:(half + 1) * 512], bank(half), AF.Ln, bias=epsT, scale=1.0 / 128)
                    yield
                f.act(rnf, rnf, AF.Exp, scale=-0.5)
                yield
                f.stt("dve", rnf, osum, gnw, rnf, ALU.mult, ALU.mult)
                yield
                f.tt("dve", oT[:, h, :], rnf, oT[:, h, :], ALU.mult)
                if u == 0 and h == 0:
                    dump("os0", osum[:, 0:256])
                yield

            def run_rr(chains):
                act_ = list(chains)
                while act_:
                    for g in list(act_):
                        try:
                            next(g)
                        except StopIteration:
                            act_.remove(g)

            def limited(g, n):
                for _ in range(n):
                    try:
                        next(g)
                    except StopIteration:
                        return
                    yield

            def tiles_of(r):
                return [(0, 2 * r), (0, 2 * r + 1), (1, 7 - 2 * r), (1, 6 - 2 * r)]

            g0 = proj_gen(0, QKV[0])
            for _ in g0:
                pass
            gen_next = None
            fin_pending = None
            NR = 4
            for R in range(H * NR + 1):
                chains = []
                if R < H * NR:
                    h, r = divmod(R, NR)
                    if r == 0:
                        if u == 0 and h == 0:
                            dump("q0", QKV[0][0][:, 0:256])
                            dump("k0", QKV[0][1][:, 0:256])
                            dump("v0", QKV[0][2][:, 0:256])
                        if h + 1 < H:
                            gen_next = proj_gen(h + 1, QKV[(h + 1) % 2])
                    for ci, (d, m) in enumerate(tiles_of(r)):
                        chains.append(prep(h, m, d, QKV[h % 2], PB[ci], SB[R % 2][ci], ci))
                    if gen_next is not None:
                        chains.append(gen_next if r == NR - 1 else limited(gen_next, 20))
                if R >= 1:
                    h2, r2 = divmod(R - 1, NR)
                    if r2 == 0:
                        f.memset("pool", osums[h2 % 2], 0.0)
                        for d in range(2):
                            Sidx[d] = 0
                            if u == 1:
                                init_state(d, h2, 0)
                    tl = tiles_of(r2)
                    Bp = SB[(R - 1) % 2]
                    chains.append(scan_chain(h2, 0, [tl[0][1], tl[1][1]], [Bp[0], Bp[1]]))
                    chains.append(scan_chain(h2, 1, [tl[2][1], tl[3][1]], [Bp[2], Bp[3]]))
                if fin_pending is not None:
                    chains.append(fin_pending)
                    fin_pending = None
                run_rr(chains)
                if R >= 1 and (R - 1) % NR == NR - 1:
                    fin_pending = finalize((R - 1) // NR)
            for _ in fin_pending:
                pass

        phase_A()
        dump("modc", MODC)
        if stop_after != "A":
            for u in units:
                unit(u)
        f.emit()
        nc._arena_peak = (AP_.peak, A_B.peak, A_S.peak, A_O.peak)
    return nc


_PROG = {}


def _col(v, n):
    return np.ascontiguousarray(np.asarray(v, np.float32).reshape(n, 128).T)


def prep_inputs(inp, cores=range(8)):
    f32 = lambda a: np.asarray(a, np.float32)
    w_in = f32(inp["w_in"])[0]
    perm = list(range(5152, 6176)) + list(range(4096, 4128))
    for h in range(H):
        for t in range(4):
            perm += list(range(t * 1024 + h * 128, t * 1024 + (h + 1) * 128))
    perm += list(range(4128, 5152))
    w_in_r = np.ascontiguousarray(w_in[:, perm])
    permu = []
    for j in range(NJ):
        permu += list(range(j * 128, (j + 1) * 128)) + list(range(DFF + j * 128, DFF + (j + 1) * 128))
    w_up_r = np.ascontiguousarray(f32(inp["w_up"])[0][:, permu])
    fcw = f32(inp["ffn_conv_w"])[0].reshape(9, 2 * DFF)[:, permu]
    fcw = np.ascontiguousarray(fcw.reshape(9, 88, 128).transpose(2, 1, 0))
    fcb = np.ascontiguousarray(f32(inp["ffn_conv_b"])[0][permu].reshape(88, 128).T)
    caw = f32(inp["conv_a_w"])[0, 0]
    caw = np.ascontiguousarray(caw.reshape(5, 3, 8, 128).transpose(3, 2, 1, 0))
    c0, c1 = _make_consts()
    shared = dict(
        cst0=c0, cst1=c1,
        w_ada=np.ascontiguousarray(f32(inp["w_ada"])[0]),
        b_ada=np.ascontiguousarray(f32(inp["b_ada"])[0][None, :]),
        w_in=w_in_r,
        w_out=np.ascontiguousarray(f32(inp["w_out"])[0]),
        w_up=w_up_r,
        w_down=np.ascontiguousarray(f32(inp["w_down"])[0]),
        sgw=np.ascontiguousarray(f32(inp["sgu_w"])[0].transpose(2, 0, 1)),
    )
    row = np.zeros((1, NROW), np.float32)
    row[0, RW_DTB:RW_DTB + 16] = f32(inp["dt_bias"])[0].reshape(16)
    row[0, RW_ALOG:RW_ALOG + 16] = f32(inp["a_log"])[0].reshape(16)
    row[0, RW_SNW:RW_SNW + 1024] = f32(inp["sgu_norm_w"])[0]
    row[0, RW_SGB:RW_SGB + 1024] = f32(inp["sgu_b"])[0].reshape(1024)
    shared["rowp"] = row
    prm0 = np.zeros((128, NPRM), np.float32)
    prm0[:, PR_N1:PR_N1 + 16] = _col(inp["norm1_w"][0], 16)
    prm0[:, PR_N2:PR_N2 + 16] = _col(inp["norm2_w"][0], 16)
    prm0[:, PR_FN:PR_FN + 16] = _col(inp["final_norm_w"], 16)
    prm0[:, PR_GN] = f32(inp["gdn_norm_w"])[0]
    prm0[:, PR_CA:PR_CA + 120] = caw.reshape(128, 120)
    prm0[:, PR_FW:PR_FW + 792] = fcw.reshape(128, 792)
    prm0[:, PR_FB:PR_FB + 88] = fcb
    xp = f32(inp["x_prompt"])
    xs = f32(inp["x_sample"])
    sf = f32(inp["state_fwd"])
    sb = f32(inp["state_bwd"])
    cc = f32(inp["c"])
    cctx = f32(inp["c_ctx"])
    maps = []
    for i in cores:
        m = dict(shared)
        xc = np.stack([xp[4 * i:4 * i + 4].reshape(T, D), xs[i]], axis=0)
        m["xT"] = np.ascontiguousarray(xc.transpose(0, 2, 1))
        prm = prm0.copy()
        prm[:, PR_C:PR_C + 16] = _col(cctx, 16)
        prm[:, PR_C + 16:PR_C + 32] = _col(cc[i], 16)
        m["prm"] = prm
        s0 = np.stack([sf[i, 0], sb[i, 0]], axis=0)
        m["s0"] = np.ascontiguousarray(s0.transpose(2, 0, 1, 3))
        maps.append(m)
    return maps


def kernel(**inputs):
    if "nc" not in _PROG:
        _PROG["nc"] = build_program()
    nc = _PROG["nc"]
    maps = prep_inputs(inputs)
    res = run_bass_kernel_spmd(nc, maps, core_ids=list(range(8)))
    r = res.results
    y_prompt = np.concatenate([r[i]["y"][0].reshape(4, 256, D) for i in range(8)], axis=0).astype(np.float32)
    y_sample = np.stack([r[i]["y"][1] for i in range(8)], axis=0).astype(np.float32)
    nsf = np.concatenate([r[i]["nsf"][:, None] for i in range(8)], axis=0).astype(np.float32)
    nsb = np.concatenate([r[i]["nsb"][:, None] for i in range(8)], axis=0).astype(np.float32)
    return (y_prompt, y_sample, nsf, nsb)
```

```python
import contextlib
import numpy as np
import concourse.bass as bass
import concourse.mybir as mybir
from concourse.bass_utils import run_bass_kernel_spmd

F32 = mybir.dt.float32
BF16 = mybir.dt.bfloat16
AF = mybir.ActivationFunctionType
ALU = mybir.AluOpType
AX = mybir.AxisListType
_DT_SIZE = {F32: 4, BF16: 2, mybir.dt.int32: 4}


class Op:
    __slots__ = ("eng", "fn", "deps", "dma_key", "dma_val", "milestone", "ms_no")

    def __init__(self, eng, fn, dma_key=None):
        self.eng = eng
        self.fn = fn
        self.deps = set()
        self.dma_key = dma_key
        self.dma_val = 0
        self.milestone = False
        self.ms_no = 0


def _region(ap):
    t = ap.tensor
    aps = ap.ap
    off = int(ap.offset)
    dsz = _DT_SIZE.get(ap.dtype, 4)
    sp = str(ap.space)
    if sp in ("SB", "PSUM"):
        pstep = aps[0][0]
        if pstep == 0:
            row = 1
            for s in list(t.shape)[1:]:
                row *= int(s)
            pstep = row * _DT_SIZE.get(t.dtype, 4) // dsz
        plo = off // pstep
        flo = off % pstep
        phi = plo + aps[0][1]
        span = 1
        for st, cnt in aps[1:]:
            span += abs(st) * (cnt - 1)
        return (t.name, plo, phi, flo * dsz, (flo + span) * dsz)
    span = 1
    for st, cnt in aps:
        span += abs(st) * (cnt - 1)
    return (t.name, 0, 1, off * dsz, (off + span) * dsz)


class Rec:
    __slots__ = ("plo", "phi", "lo", "hi", "writer", "readers")

    def __init__(self, plo, phi, lo, hi, writer):
        self.plo, self.phi, self.lo, self.hi = plo, phi, lo, hi
        self.writer = writer
        self.readers = []


class Fw:
    ENGS = ("pe", "act", "dve", "pool", "sp")

    def __init__(self, nc):
        self.nc = nc
        self.q = {e: [] for e in self.ENGS}
        self.regions = {}
        self.untracked = set()
        self.dma_cnt = {}
        self.psum_last = {}

    def _track(self, ap, op, is_write):
        name, plo, phi, lo, hi = _region(ap)
        if name in self.untracked:
            return
        if str(ap.space) == "PSUM":
            for b in range(lo // 2048, (hi - 1) // 2048 + 1):
                d = self.psum_last.setdefault(b, {})
                for e2, o2 in d.items():
                    if e2 != op.eng:
                        op.deps.add(o2)
                d[op.eng] = op
        recs = self.regions.setdefault(name, [])
        deps = op.deps
        if is_write:
            keep = []
            for r in recs:
                if r.plo < phi and plo < r.phi and r.lo < hi and lo < r.hi:
                    if r.writer is not None:
                        deps.add(r.writer)
                    for rd, a_, b_, c_, d_ in r.readers:
                        if a_ < phi and plo < b_ and c_ < hi and lo < d_:
                            deps.add(rd)
                    if plo <= r.plo and r.phi <= phi and lo <= r.lo and r.hi <= hi:
                        continue
                    keep.append(r)
                else:
                    keep.append(r)
            keep.append(Rec(plo, phi, lo, hi, op))
            self.regions[name] = keep
        else:
            cover = False
            for r in recs:
                if r.plo < phi and plo < r.phi and r.lo < hi and lo < r.hi:
                    if r.writer is not None:
                        deps.add(r.writer)
                    r.readers.append((op, plo, phi, lo, hi))
                    if r.plo <= plo and phi <= r.phi and r.lo <= lo and hi <= r.hi:
                        cover = True
            if not cover:
                r = Rec(plo, phi, lo, hi, None)
                r.readers.append((op, plo, phi, lo, hi))
                recs.append(r)

    def op(self, eng, fn, reads=(), writes=(), dma_key=None):
        o = Op(eng, fn, dma_key)
        for ap in reads:
            if ap is not None and not isinstance(ap, (int, float)):
                self._track(ap, o, False)
        for ap in writes:
            self._track(ap, o, True)
        o.deps.discard(o)
        if dma_key is not None:
            c = self.dma_cnt.get(dma_key, 0) + 16
            self.dma_cnt[dma_key] = c
            o.dma_val = c
        self.q[eng].append(o)
        return o

    def mm(self, out, lhsT, rhs, start=True, stop=True):
        return self.op("pe", lambda e: e.matmul(out, lhsT, rhs, start=start, stop=stop),
                       reads=[lhsT, rhs], writes=[out])

    def tr(self, out, in_, ident):
        return self.op("pe", lambda e: e.transpose(out, in_, ident), reads=[in_, ident], writes=[out])

    def act(self, out, in_, func, bias=None, scale=None, accum_out=None):
        kw = {}
        rd = [in_]
        if bias is not None:
            kw["bias"] = bias
            rd.append(bias)
        if scale is not None:
            kw["scale"] = scale
            rd.append(scale)
        wr = [out]
        if accum_out is not None:
            kw["accum_out"] = accum_out
            wr.append(accum_out)
        return self.op("act", lambda e: e.activation(out, in_, func, **kw), reads=rd, writes=wr)

    def tt(self, eng, out, in0, in1, op):
        return self.op(eng, lambda e: e.tensor_tensor(out, in0, in1, op), reads=[in0, in1], writes=[out])

    def ts(self, eng, out, in0, s1, s2=None, op0=ALU.mult, op1=None):
        kw = {}
        if op1 is not None:
            kw["op1"] = op1
        return self.op(eng, lambda e: e.tensor_scalar(out, in0, s1, s2, op0, **kw),
                       reads=[in0, s1, s2], writes=[out])

    def stt(self, eng, out, in0, scalar, in1, op0, op1):
        return self.op(eng, lambda e: e.scalar_tensor_tensor(out, in0, scalar, in1, op0, op1),
                       reads=[in0, scalar, in1], writes=[out])

    def copy(self, eng, out, in_):
        if eng == "act":
            return self.op("act", lambda e: e.copy(out, in_), reads=[in_], writes=[out])
        return self.op(eng, lambda e: e.tensor_copy(out, in_), reads=[in_], writes=[out])

    def memset(self, eng, ap, val):
        return self.op(eng, lambda e: e.memset(ap, val), writes=[ap])

    def recip(self, out, in_):
        return self.op("dve", lambda e: e.reciprocal(out, in_), reads=[in_], writes=[out])

    def reduce(self, eng, out, in_, op=ALU.add, axis=AX.X):
        return self.op(eng, lambda e: e.tensor_reduce(out, in_, axis, op), reads=[in_], writes=[out])

    def dma(self, queue, out, in_, key):
        return self.op(queue, lambda e: e.dma_start(out, in_), reads=[in_], writes=[out], dma_key=key)

    def emit(self):
        nc = self.nc
        for e in self.ENGS:
            for o in self.q[e]:
                for d in o.deps:
                    if d.dma_key is None and not (d.eng == "pe" and o.eng == "pe"):
                        d.milestone = True
        for e in self.ENGS:
            n = 0
            for o in self.q[e]:
                if o.milestone:
                    n += 1
                    o.ms_no = n
        with contextlib.ExitStack() as st:
            esem = {e: st.enter_context(nc.semaphore("s_" + e)) for e in self.ENGS}
            dsem = {k: st.enter_context(nc.semaphore("d_%s" % (k,))) for k in self.dma_cnt}
            block = st.enter_context(nc.Block())
            fw = self

            def run(engname, eng):
                waited = {}
                for o in fw.q[engname]:
                    need = {}
                    for d in o.deps:
                        if d.dma_key is not None:
                            s, v = ("d", d.dma_key), d.dma_val
                        else:
                            if d.eng == "pe" and engname == "pe":
                                continue
                            s, v = ("e", d.eng), d.ms_no
                        if waited.get(s, 0) >= v:
                            continue
                        if need.get(s, 0) < v:
                            need[s] = v
                    for s, v in need.items():
                        eng.wait_ge(dsem[s[1]] if s[0] == "d" else esem[s[1]], v)
                        waited[s] = v
                    ins = o.fn(eng)
                    if o.dma_key is not None:
                        ins.then_inc(dsem[o.dma_key], 16)
                    elif o.milestone:
                        ins.then_inc(esem[engname], 1)
                if engname == "sp":
                    for k, v in fw.dma_cnt.items():
                        if waited.get(("d", k), 0) < v:
                            eng.wait_ge(dsem[k], v)

            @block.tensor
            def _(eng):
                run("pe", eng)

            @block.scalar
            def _(eng):
                run("act", eng)

            @block.vector
            def _(eng):
                run("dve", eng)

            @block.gpsimd
            def _(eng):
                run("pool", eng)

            @block.sync
            def _(eng):
                run("sp", eng)


class Arena:
    def __init__(self, t, words, base=0):
        self.t = t
        self.words = words
        self.base = base
        self.top = 0
        self.peak = 0

    def alloc(self, shape, dt=F32):
        shape = [int(s) for s in (shape if isinstance(shape, (list, tuple)) else [shape])]
        n = int(np.prod(shape))
        nbytes = n * (2 if dt == BF16 else 4)
        w = (nbytes + 31) // 32 * 8
        off = self.base + self.top
        self.top += w
        self.peak = max(self.peak, self.top)
        assert self.top <= self.words, "arena overflow: %d > %d words" % (self.top, self.words)
        v = self.t[:, off:off + (nbytes + 3) // 4]
        if dt == BF16:
            v = v.bitcast(BF16)
        if len(shape) > 1:
            names = ["a%d" % i for i in range(len(shape))]
            pat = "p (%s) -> p %s" % (" ".join(names), " ".join(names))
            v = v.rearrange(pat, **{names[i]: shape[i] for i in range(1, len(shape))})
        return v

    def mark(self):
        return self.top

    def release(self, m):
        self.top = m


class WStream:
    def __init__(self, f, slots, name, queue="pool", depth=1):
        self.f, self.slots, self.name, self.queue, self.depth = f, slots, name, queue, depth
        self.loads = []
        self.issued = 0
        self.next = 0

    def plan(self, src, view=None):
        self.loads.append((src, view))

    def ahead(self, n=1):
        tgt = min(self.next - 1 + n, len(self.loads) - 1)
        ns = len(self.slots)
        while self.issued <= tgt:
            j = self.issued
            src, view = self.loads[j]
            slot = self.slots[j % ns]
            self.f.dma(self.queue, view(slot) if view else slot, src, "%s%d" % (self.name, j % ns))
            self.issued += 1

    def get(self, depth=None):
        i = self.next
        self.next += 1
        n = len(self.slots)
        depth = self.depth if depth is None else depth
        while self.issued <= min(i + depth, len(self.loads) - 1):
            j = self.issued
            src, view = self.loads[j]
            slot = self.slots[j % n]
            self.f.dma(self.queue, view(slot) if view else slot, src, "%s%d" % (self.name, j % n))
            self.issued += 1
        src, view = self.loads[i]
        slot = self.slots[i % n]
        return view(slot) if view else slot


D = 2048
KC = 16
T = 1024
NT = 8
H = 8
P_IN = 6176
DFF = 5632
NJ = 44
NG = 4
JG = NJ // NG
SV0, AB0, HD0, U0 = 0, 1024, 1056, 5152
EPS = 1e-6
NEG = -30000.0

C_ID, C_ONE = 0, 128
NCST0 = 256
C_TRIF, C_TRIB, C_BLK, C_CH0, C_CH1, C_NMF, C_NMB = 0, 128, 256, 384, 512, 640, 896
NCST1 = 1152
PR_C, PR_N1, PR_N2, PR_FN, PR_GN, PR_CA, PR_FW, PR_FB = 0, 32, 48, 64, 80, 81, 201, 993
NPRM = 1081
RW_DTB, RW_ALOG, RW_SNW, RW_SGB = 0, 16, 32, 1056
NROW = 2080


def _make_consts():
    c0 = np.zeros((128, NCST0), np.float32)
    c0[:, C_ID:C_ID + 128] = np.eye(128)
    c0[:, C_ONE:C_ONE + 128] = 1.0
    c = np.zeros((128, NCST1), np.float32)
    idx = np.arange(128)
    same = (idx[:, None] // 64) == (idx[None, :] // 64)
    c[:, C_TRIF:C_TRIF + 128] = (same & (idx[:, None] <= idx[None, :]))
    c[:, C_TRIB:C_TRIB + 128] = (same & (idx[:, None] >= idx[None, :]))
    c[:, C_BLK:C_BLK + 128] = same
    c[:, C_CH0:C_CH0 + 128] = (idx[:, None] < 64)
    c[:, C_CH1:C_CH1 + 128] = (idx[:, None] >= 64)
    nmf = np.full((128, 2, 128), NEG, np.float32)
    nmf[:, 0][same & (idx[None, :] >= idx[:, None])] = 0.0
    nmf[:, 1][same & (idx[None, :] > idx[:, None])] = 0.0
    nmb = np.full((128, 2, 128), NEG, np.float32)
    nmb[:, 0][same & (idx[None, :] <= idx[:, None])] = 0.0
    nmb[:, 1][same & (idx[None, :] < idx[:, None])] = 0.0
    c[:, C_NMF:C_NMF + 256] = nmf.reshape(128, 256)
    c[:, C_NMB:C_NMB + 256] = nmb.reshape(128, 256)
    return c0, c


def build_program(stop_after=None, dumps=(), units=(0, 1)):
    nc = bass.Bass("TRN2", target_bir_lowering=False)
    din = lambda n, s: nc.dram_tensor(n, list(s), F32, kind="ExternalInput").ap()
    dout = lambda n, s: nc.dram_tensor(n, list(s), F32, kind="ExternalOutput").ap()
    xT_d = din("xT", [2, D, T])
    cst0_d = din("cst0", [128, NCST0])
    cst1_d = din("cst1", [128, NCST1])
    prm_d = din("prm", [128, NPRM])
    row_d = din("rowp", [1, NROW])
    s0_d = din("s0", [128, 2, H, 128])
    sgw_d = din("sgw", [128, 8, 128])
    w_ada_d = din("w_ada", [D, 6 * D])
    b_ada_d = din("b_ada", [1, 6 * D])
    w_in_d = din("w_in", [D, P_IN])
    w_out_d = din("w_out", [D, D])
    w_up_d = din("w_up", [D, 2 * DFF])
    w_dn_d = din("w_down", [DFF, D])
    y_d = dout("y", [2, T, D])
    nsf_d = dout("nsf", [4, H, 128, 128])
    nsb_d = dout("nsb", [4, H, 128, 128])
    dump_d = {n: dout("dbg_" + n, s) for n, s in dumps}

    f = Fw(nc)
    f.untracked |= {"xT", "cst0", "cst1", "prm", "rowp", "s0", "sgw", "w_ada", "b_ada", "w_in", "w_out", "w_up", "w_down"}
    dbgctr = [0]

    with contextlib.ExitStack() as st:
        ARW = 52000
        arena_t = st.enter_context(nc.sbuf_tensor("arena", [128, ARW], F32))
        ps = st.enter_context(nc.psum_tensor("ps", [128, 8, 512], F32))

        def bank(b, n=1):
            if n == 1:
                return ps[:, b, :]
            return ps[:, b:b + n, :].rearrange("p a b -> p (a b)")

        W_HT, W_OT, W_XR, W_SC = 8192, 8192, 16384, 8704
        W_P = ARW - (W_HT + W_OT + W_XR + W_SC)
        AP_ = Arena(arena_t, W_P, 0)
        o_ht = W_P
        o_ot = o_ht + W_HT
        o_xr = o_ot + W_OT
        o_sc = o_xr + W_XR
        hT = arena_t[:, o_ht:o_ht + W_HT].bitcast(BF16).rearrange("p (c t) -> p c t", c=16)
        oT = arena_t[:, o_ot:o_ot + W_OT].bitcast(BF16).rearrange("p (c t) -> p c t", c=16)
        XR = arena_t[:, o_xr:o_xr + W_XR].rearrange("p (c t) -> p c t", c=16)
        A_O = Arena(arena_t, W_OT, o_ot)
        A_B = Arena(arena_t, W_XR + W_SC, o_xr)
        A_S = Arena(arena_t, W_SC, o_sc)

        cst0 = AP_.alloc([NCST0])
        prm = AP_.alloc([NPRM])
        identb = AP_.alloc([128], BF16)
        onesb = AP_.alloc([128], BF16)
        epsT = AP_.alloc([1])
        MODC = AP_.alloc([2, 6, 16])
        GCOL = AP_.alloc([2, 2, 16])
        f.dma("sp", cst0, cst0_d, "cst0")
        f.dma("sp", prm, prm_d, "prm")
        f.memset("dve", epsT, EPS)
        ident = cst0[:, C_ID:C_ID + 128]
        ones = cst0[:, C_ONE:C_ONE + 128]
        f.copy("dve", identb, ident)
        f.copy("dve", onesb, ones)

        def dump(name, ap_sb):
            if name in dump_d:
                dbgctr[0] += 1
                f.dma("pool" if ap_sb.dtype == BF16 else "sp", dump_d[name], ap_sb, "dbg%d" % dbgctr[0])

        NWS = 2
        wslot = [AP_.alloc([16, 512], BF16) for _ in range(NWS)]
        WS = WStream(f, wslot, "ws")
        wsrc = lambda ap: ap.rearrange("(c p) n -> p c n", p=128)
        for blk in range(24):
            WS.plan(wsrc(w_ada_d[:, blk * 512:(blk + 1) * 512]))
        for u_ in units:
            WS.plan(wsrc(w_in_d[:, SV0:SV0 + 512]))
            WS.plan(wsrc(w_in_d[:, SV0 + 512:SV0 + 1024]))
            for g_ in range(2):
                WS.plan(wsrc(w_in_d[:, U0 + g_ * 512:U0 + (g_ + 1) * 512]))
            for h_ in range(H):
                WS.plan(wsrc(w_in_d[:, HD0 + h_ * 512:HD0 + (h_ + 1) * 512]))
            for c_ in range(4):
                WS.plan(wsrc(w_out_d[:, c_ * 512:(c_ + 1) * 512]))
            for j_ in range(NJ // 2):
                WS.plan(wsrc(w_up_d[:, j_ * 512:(j_ + 1) * 512]))
        WD = None

        def phase_A():
            A_B.release(0)
            sc = A_B.alloc([2, 16])
            sbc = A_B.alloc([2, 16, 128], BF16)
            f.act(sc, prm[:, PR_C:PR_C + 32].rearrange("p (u c) -> p u c", u=2), AF.Silu)
            f.copy("dve", sbc, sc.unsqueeze(3).broadcast_to([128, 2, 16, 128]))
            bb = [A_B.alloc([2048]) for _ in range(2)]
            tmpm = [A_B.alloc([2048]) for _ in range(2)]
            scr = A_B.alloc([16, 128])
            for s in range(6):
                f.dma("sp", bb[s % 2], b_ada_d[:, s * D:(s + 1) * D].partition_broadcast(128), "bb%d" % (s % 2))
                for cb in range(4):
                    blk = s * 4 + cb
                    wa = WS.get()
                    for u in range(2):
                        pb = bank((blk * 2 + u) % 8)
                        for kk in range(KC):
                            f.mm(pb, sbc[:, u, kk, :], wa[:, kk, :], start=(kk == 0), stop=(kk == KC - 1))
                        f.tt("dve", tmpm[u][:, cb * 512:(cb + 1) * 512], pb, bb[s % 2][:, cb * 512:(cb + 1) * 512], ALU.add)
                for u in range(2):
                    f.tt("dve", scr, tmpm[u].rearrange("p (c j) -> p c j", c=16),
                         ident.unsqueeze(1).broadcast_to([128, 16, 128]), ALU.mult)
                    f.reduce("dve", MODC[:, u, s, :], scr)
            for u in range(2):
                f.stt("dve", GCOL[:, u, 0, :], MODC[:, u, 1, :], 1.0, prm[:, PR_N1:PR_N1 + 16], ALU.add, ALU.mult)
                f.stt("dve", GCOL[:, u, 1, :], MODC[:, u, 4, :], 1.0, prm[:, PR_N2:PR_N2 + 16], ALU.add, ALU.mult)

        def rstd_rows(src, AR, load_fn=None):
            rr = AR.alloc([1024])
            sq = [AR.alloc([1024], BF16) for _ in range(2)]
            for c in range(KC):
                if load_fn is not None:
                    load_fn(c)
                f.act(sq[c % 2], src[:, c, :], AF.Square)
                for half in range(2):
                    f.mm(bank(half), onesb, sq[c % 2][:, half * 512:(half + 1) * 512], start=(c == 0), stop=(c == KC - 1))
            f.act(rr, bank(0, 2), AF.Ln, bias=epsT, scale=1.0 / D)
            f.act(rr, rr, AF.Exp, scale=-0.5)
            return rr

        def modulate_to(u, which, src, rr, AR, dst):
            tmp = [AR.alloc([1024]) for _ in range(2)]
            sec_shift = 0 if which == 0 else 3
            for c in range(KC):
                t = tmp[c % 2]
                f.stt("dve", t, src[:, c, :], GCOL[:, u, which, c:c + 1], rr, ALU.mult, ALU.mult)
                f.act(dst[:, c, :], t, AF.Identity, bias=MODC[:, u, sec_shift, c:c + 1])

        def unit(u):
            nseq, L = (4, 256) if u == 0 else (1, 1024)

            A_S.release(0)

            def load_x(c):
                f.dma("sp", XR[:, c, :], xT_d[u, c * 128:(c + 1) * 128, :], "xT%d" % c)
            rr = rstd_rows(XR, A_S, load_x)
            modulate_to(u, 0, XR, rr, A_S, hT)
            if u == 0:
                dump("hT", hT[:, :, 0:256])
            if stop_after == "B":
                return

            A_B.release(0)
            cst1 = A_B.alloc([NCST1])
            f.dma("sp", cst1, cst1_d, "cst1")
            GA = A_B.alloc([8, 16, 2])
            BET = A_B.alloc([8, 16])
            BEG = A_B.alloc([8, 16])
            EK = A_B.alloc([8, 16])
            GLB = A_B.alloc([8, 2, 16])
            mC1 = A_B.mark()
            svn = A_B.alloc([8, 1024], BF16)
            rowb = A_B.alloc([32])
            snw_b = A_B.alloc([1024])
            f.dma("sp", rowb, row_d[:, 0:32].partition_broadcast(128), "rwa")
            f.dma("sp", snw_b, row_d[:, RW_SNW:RW_SNW + 1024].partition_broadcast(128), "rwb")
            negA = A_B.alloc([16])
            f.act(negA, rowb[:, 16:32], AF.Exp)
            f.ts("dve", negA, negA, -1.0, None, op0=ALU.mult)
            wab = A_B.alloc([16, 32], BF16)
            f.dma("pool", wab, w_in_d[:, AB0:AB0 + 32].rearrange("(c p) n -> p c n", p=128), "wab")
            ws0 = WS.get(depth=1)
            ws1 = WS.get(depth=0)
            ABR = A_B.alloc([8, 32])
            gsv = [A_B.alloc([1024]) for _ in range(2)]
            junk = A_B.alloc([1024], BF16)
            ss2 = A_B.alloc([8])
            rs2 = A_B.alloc([8])
            for m in range(NT):
                b0 = (m % 2) * 3
                for kk in range(KC):
                    lt = hT[:, kk, m * 128:(m + 1) * 128]
                    f.mm(bank(b0), lt, ws0[:, kk, :], start=(kk == 0), stop=(kk == KC - 1))
                    f.mm(bank(b0 + 1), lt, ws1[:, kk, :], start=(kk == 0), stop=(kk == KC - 1))
                    f.mm(bank(b0 + 2)[:, 0:32], lt, wab[:, kk, :], start=(kk == 0), stop=(kk == KC - 1))
                g = gsv[m % 2]
                f.act(g, bank(b0, 2), AF.Gelu_apprx_tanh)
                f.act(junk, g, AF.Square, accum_out=ss2[:, m:m + 1])
                f.copy("dve", ABR[:, m, :], bank(b0 + 2)[:, 0:32])
                f.act(rs2[:, m:m + 1], ss2[:, m:m + 1], AF.Sqrt, bias=epsT, scale=1.0 / 1024)
                f.recip(rs2[:, m:m + 1], rs2[:, m:m + 1])
                f.stt("dve", svn[:, m, :], g, rs2[:, m:m + 1], snw_b, ALU.mult, ALU.mult)
            WS.ahead(1)
            zt = A_B.alloc([8, 16])
            GL = A_B.alloc([8, 16])
            LNB = A_B.alloc([8, 16])
            EGt = A_B.alloc([8, 16])
            f.tt("dve", zt, ABR[:, :, 0:16], rowb[:, 0:16].unsqueeze(1).broadcast_to([128, 8, 16]), ALU.add)
            f.act(zt, zt, AF.Exp)
            f.act(zt, zt, AF.Ln, bias=1.0)
            f.tt("dve", GL, zt, negA.unsqueeze(1).broadcast_to([128, 8, 16]), ALU.mult)
            f.act(BET, ABR[:, :, 16:32], AF.Sigmoid)
            f.act(LNB, BET, AF.Ln)
            pc = bank(7).rearrange("p (m c) -> p m c", m=8)
            for m in range(NT):
                f.mm(pc[:, m, 0:8], cst1[:, C_TRIF:C_TRIF + 128], GL[:, m, 0:8])
                f.mm(pc[:, m, 8:16], cst1[:, C_TRIB:C_TRIB + 128], GL[:, m, 8:16])
                f.mm(pc[:, m, 16:32], cst1[:, C_BLK:C_BLK + 128], GL[:, m, :])
                f.mm(pc[:, m, 32:48], cst1[:, C_CH0:C_CH0 + 128], GL[:, m, :])
                f.mm(pc[:, m, 48:64], cst1[:, C_CH1:C_CH1 + 128], GL[:, m, :])
            f.copy("dve", GA[:, :, :, 0], pc[:, :, 0:16])
            f.tt("dve", GA[:, :, :, 1], pc[:, :, 0:16], LNB, ALU.add)
            f.act(EGt, pc[:, :, 0:16], AF.Exp)
            f.tt("dve", BEG, EGt, BET, ALU.mult)
            f.tt("dve", zt, pc[:, :, 16:32], GA[:, :, :, 0], ALU.subtract)
            f.act(EK, zt, AF.Exp)
            f.act(GLB, pc[:, :, 32:64].rearrange("p m (a c) -> p m a c", a=2), AF.Exp)
            if u == 0:
                dump("svn", svn[:, 0:2, :])
                dump("GL", GL)
                dump("BET", BET)
            if stop_after == "C1":
                return

            sgwT = A_B.alloc([8, 128], BF16)
            f.dma("pool", sgwT, sgw_d, "sgw")
            sgb = A_B.alloc([8, 128])
            f.dma("sp", sgb, row_d[:, RW_SGB:RW_SGB + 1024].partition_broadcast(128), "rwc")
            ug = [A_B.alloc([1024]) for _ in range(2)]
            tmpg = A_B.alloc([1024])
            wu = None
            for g in range(8):
                if g % 4 == 0:
                    wu = WS.get()
                bu = (g % 2) * 4
                for half in range(2):
                    for kk in range(KC):
                        f.mm(bank(bu + half), wu[:, kk, (g % 4) * 128:(g % 4 + 1) * 128],
                             hT[:, kk, half * 512:(half + 1) * 512], start=(kk == 0), stop=(kk == KC - 1))
                f.act(ug[g % 2], bank(bu, 2), AF.Gelu_apprx_tanh)
                pm = bank(bu + 2, 2)
                for n in range(NT):
                    f.mm(pm[:, n * 128:(n + 1) * 128], svn[:, n, g * 128:(g + 1) * 128], sgwT[:, g, :])
                f.tt("dve", tmpg.rearrange("p (n i) -> p n i", n=8), pm.rearrange("p (n i) -> p n i", n=8),
                     sgb[:, g, :].unsqueeze(1).broadcast_to([128, 8, 128]), ALU.add)
                f.tt("pool", oT[:, 8 + g, :], tmpg, ug[g % 2], ALU.mult)
            if u == 0:
                dump("oTb", oT[:, 8:16, 0:256])
            A_B.release(mC1)
            if stop_after == "SGU":
                return

            heads(u, cst1, GA, BET, BEG, EK, GLB, nseq, L)
            if u == 0:
                dump("oTa", oT[:, 0:8, 0:256])
            if stop_after == "HEADS":
                return

            A_S.release(0)
            xq = [A_S.alloc([512]) for _ in range(3)]
            i = 0
            wo = None
            for c in range(KC):
                if c % 4 == 0:
                    wo = WS.get()
                for half in range(2):
                    xt = xq[i % 3]
                    f.dma("sp", xt, xT_d[u, c * 128:(c + 1) * 128, half * 512:(half + 1) * 512], "xq%d" % (i % 3))
                    pb = bank(4 + i % 4)
                    for kk in range(KC):
                        f.mm(pb, wo[:, kk, (c % 4) * 128:(c % 4 + 1) * 128], oT[:, kk, half * 512:(half + 1) * 512],
                             start=(kk == 0), stop=(kk == KC - 1))
                    f.stt("dve", XR[:, c, half * 512:(half + 1) * 512], pb, MODC[:, u, 2, c:c + 1], xt, ALU.mult, ALU.add)
                    i += 1
            if u == 0:
                dump("x1T", XR[:, :, 0:256])
            if stop_after == "D":
                return
            A_S.release(0)
            rr = rstd_rows(XR, A_S)
            modulate_to(u, 1, XR, rr, A_S, hT)
            h2T = hT
            if stop_after == "D2":
                return

            A_O.release(0)
            A_S.release(0)
            actT = A_O.alloc([JG, 1024], BF16)
            PADW = 4 * 258 if u == 0 else 18 * 66
            raws = [A_O.alloc([PADW]) for _ in range(2)]
            for r in raws:
                f.memset("pool", r, 0.0)
            accs2 = [[A_S.alloc([1024]) for _ in range(2)] for _ in range(2)]
            wds = [A_S.alloc([JG, 256], BF16) for _ in range(2)]
            WDs = WStream(f, wds, "wd")
            for grp_ in range(NG):
                for cb_ in range(8):
                    WDs.plan(w_dn_d[grp_ * JG * 128:(grp_ + 1) * JG * 128, cb_ * 256:(cb_ + 1) * 256]
                             .rearrange("(c p) n -> p c n", p=128))
            fw_ = prm[:, PR_FW:PR_FW + 792].rearrange("p (c t) -> p c t", c=88)
            fb_ = prm[:, PR_FB:PR_FB + 88]

            def views(raw, acc):
                if u == 0:
                    return raw.rearrange("p (s l) -> p s l", s=4), acc.rearrange("p (s l) -> p s l", s=4)
                return raw.rearrange("p (r c) -> p r c", r=18), acc.rearrange("p (r c) -> p r c", r=16)

            def win(rv, a, b):
                if u == 0:
                    return rv[:, :, b:b + 256]
                return rv[:, a:a + 16, b:b + 64]

            def inner(rv):
                if u == 0:
                    return rv[:, :, 1:257]
                return rv[:, 1:17, 1:65]

            taps = [(1, 0), (1, 1), (1, 2)] if u == 0 else [(a, b) for a in range(3) for b in range(3)]
            upctr = 0
            dctr = 0
            ectr = 0
            wu_ = None
            def emit_down(grp):
                nonlocal ectr
                for cb in range(8):
                    wd = WDs.get()
                    for dc in range(2):
                        c = cb * 2 + dc
                        for half in range(2):
                            pb = bank(6 + (ectr % 2))
                            ectr += 1
                            for jj_ in range(JG):
                                f.mm(pb, wd[:, jj_, dc * 128:(dc + 1) * 128], actT[:, jj_, half * 512:(half + 1) * 512],
                                     start=(jj_ == 0), stop=(jj_ == JG - 1))
                            xs = XR[:, c, half * 512:(half + 1) * 512]
                            f.stt("dve", xs, pb, MODC[:, u, 5, c:c + 1], xs, ALU.mult, ALU.add)

            deferred = []
            for j in range(NJ):
                grp, jj = divmod(j, JG)
                if j % 2 == 0:
                    wu_ = WS.get()
                accs = accs2[j % 2]
                for t2 in range(2):
                    cj = 2 * j + t2
                    col0 = (j % 2) * 256 + t2 * 128
                    b0 = (upctr % 3) * 2
                    upctr += 1
                    for half in range(2):
                        for kk in range(KC):
                            f.mm(bank(b0 + half), wu_[:, kk, col0:col0 + 128],
                                 h2T[:, kk, half * 512:(half + 1) * 512], start=(kk == 0), stop=(kk == KC - 1))
                    rv, av = views(raws[t2], accs[t2])
                    src = bank(b0, 2)
                    if u == 0:
                        src = src.rearrange("p (s l) -> p s l", s=4)
                    else:
                        src = src.rearrange("p (r c) -> p r c", r=16)
                    f.copy("act", inner(rv), src)
                    for ti, (a, b) in enumerate(taps):
                        wsc = fw_[:, cj, a * 3 + b:a * 3 + b + 1]
                        if ti == 0:
                            f.act(av, win(rv, a, b), AF.Identity, scale=wsc, bias=fb_[:, cj:cj + 1])
                        else:
                            f.stt("dve", av, win(rv, a, b), wsc, av, ALU.mult, ALU.add)
                f.act(accs[0], accs[0], AF.Silu)
                mult = (lambda jj_, a_: (lambda: f.tt("dve" if u == 1 else "pool", actT[:, jj_, :], a_[0], a_[1], ALU.mult)))(jj, accs)
                if grp > 0 and jj < 2:
                    deferred.append(mult)
                    if jj == 1:
                        emit_down(grp - 1)
                        for m_ in deferred:
                            m_()
                        deferred = []
                else:
                    mult()
                if u == 0 and j == 0:
                    dump("act0", actT[:, 0, 0:256])
            emit_down(NG - 1)
            if u == 0:
                dump("x2T", XR[:, :, 0:256])

            A_O.release(0)
            A_S.release(0)
            rr = rstd_rows(XR, A_S)
            for c in range(KC):
                f.stt("dve", XR[:, c, :], XR[:, c, :], prm[:, PR_FN + c:PR_FN + c + 1], rr, ALU.mult, ALU.mult)
            ys = [A_O.alloc([2048]) for _ in range(2)]
            for m in range(NT):
                pb = bank(4 * (m % 2), 4)
                for c in range(KC):
                    f.tr(pb[:, c * 128:(c + 1) * 128], XR[:, c, m * 128:(m + 1) * 128], ident)
                f.copy("act", ys[m % 2][:, 0:1024], pb[:, 0:1024])
                f.copy("dve", ys[m % 2][:, 1024:2048], pb[:, 1024:2048])
                f.dma("sp", y_d[u, m * 128:(m + 1) * 128, :], ys[m % 2], "ys%d" % (m % 2))

        def heads(u, cst1, GA, BET, BEG, EK, GLB, nseq, L):
            ca = prm[:, PR_CA:PR_CA + 120].rearrange("p (h t k) -> p h t k", h=8, t=3)
            gnw = prm[:, PR_GN:PR_GN + 1]
            QKV = [[A_B.alloc([1024], BF16), A_B.alloc([1024]), A_B.alloc([1024]), A_B.alloc([1024], BF16)] for _ in range(2)]
            PADW = nseq * (L + 4)
            raw = A_B.alloc([PADW])
            f.memset("pool", raw, 0.0)
            rawv = raw.rearrange("p (s l) -> p s l", s=nseq)
            acc = A_B.alloc([1024])
            accv = acc.rearrange("p (s l) -> p s l", s=nseq)
            rn = A_B.alloc([1024])
            osums = [A_B.alloc([1024]) for _ in range(2)]
            rnf = A_B.alloc([1024])
            zeroS = A_B.alloc([128])
            f.memset("pool", zeroS, 0.0)
            Sst = [[A_B.alloc([128]) for _ in range(2)] for _ in range(2)]
            Sbf = [[A_B.alloc([128], BF16) for _ in range(2)] for _ in range(2)]
            sout = [A_B.alloc([128]) for _ in range(2)]
            NM = (cst1[:, C_NMF:C_NMF + 256].rearrange("p (a b) -> p a b", a=2),
                  cst1[:, C_NMB:C_NMB + 256].rearrange("p (a b) -> p a b", a=2))

            def prep_bufs():
                return dict(E=A_B.alloc([2, 128]), R0=A_B.alloc([256]), W=A_B.alloc([3, 128]), eg=A_B.alloc([128]))

            def scan_bufs():
                return dict(AT=A_B.alloc([128], BF16), qd=A_B.alloc([128], BF16), kd=A_B.alloc([128], BF16),
                            u=A_B.alloc([128]), wT=A_B.alloc([128], BF16), vn=A_B.alloc([128], BF16))
            PB = [prep_bufs() for _ in range(4)]
            SB = [[scan_bufs() for _ in range(4)] for _ in range(2)]

            def proj_gen(h, dst):
                wh = WS.get()
                yield
                for t in range(4):
                    for half in range(2):
                        for kk in range(KC):
                            f.mm(bank(half), wh[:, kk, t * 128:(t + 1) * 128], hT[:, kk, half * 512:(half + 1) * 512],
                                 start=(kk == 0), stop=(kk == KC - 1))
                            if kk % 4 == 3:
                                yield
                    src = bank(0, 2)
                    if t == 3:
                        f.act(oT[:, h, :], src, AF.Silu)
                        yield
                        continue
                    f.copy("act", rawv[:, :, 2:2 + L], src.rearrange("p (s l) -> p s l", s=nseq))
                    yield
                    for tap in range(5):
                        wsc = ca[:, h, t, tap:tap + 1]
                        if tap == 0:
                            f.ts("dve", accv, rawv[:, :, 0:L], wsc, None, op0=ALU.mult)
                        else:
                            f.stt("dve", accv, rawv[:, :, tap:tap + L], wsc, accv, ALU.mult, ALU.add)
                        yield
                    if t == 2:
                        f.act(dst[2], acc, AF.Silu)
                        yield
                        continue
                    f.act(acc, acc, AF.Silu)
                    yield
                    f.act(rn, acc, AF.Square)
                    yield
                    for half in range(2):
                        f.mm(bank(half), ones, rn[:, half * 512:(half + 1) * 512])
                        yield
                        f.act(rn[:, half * 512:(half + 1) * 512], bank(half), AF.Ln, bias=epsT)
                        yield
                    f.act(rn, rn, AF.Exp, scale=-0.5)
                    yield
                    if t == 0:
                        f.stt("dve", dst[0], acc, float(128 ** -0.5), rn, ALU.mult, ALU.mult)
                    else:
                        f.tt("dve", dst[1], acc, rn, ALU.mult)
                        yield
                        f.copy("act", dst[3], dst[1])
                    yield

            def prep(h, m, d, qkv, B, S_, ci):
                cd = d * 8 + h
                qT = qkv[0][:, m * 128:(m + 1) * 128]
                kT = qkv[1][:, m * 128:(m + 1) * 128]
                vT = qkv[2][:, m * 128:(m + 1) * 128]
                kTb = qkv[3][:, m * 128:(m + 1) * 128]
                bk = bank(4 + ci)
                slot = [bk[:, i * 128:(i + 1) * 128] for i in range(4)]
                kqkk = bk[:, 0:256].rearrange("p (a b) -> p a b", a=2)
                ktok, vtok = slot[2], slot[3]
                f.mm(kqkk[:, 0, :], kTb, qT)
                f.mm(kqkk[:, 1, :], kTb, kTb)
                f.tr(ktok, kT, ident)
                f.tr(vtok, vT, ident)
                yield
                gc = GA[:, m, cd, 0:1]
                E = B["E"]
                f.tt("dve", E, ident.unsqueeze(1).broadcast_to([128, 2, 128]),
                     GA[:, m, cd, :].unsqueeze(2).broadcast_to([128, 2, 128]), ALU.mult)
                f.act(B["R0"][:, 0:128], vtok, AF.Identity, scale=BET[:, m, cd:cd + 1])
                yield
                f.act(B["R0"][:, 128:256], ktok, AF.Identity, scale=BEG[:, m, cd:cd + 1])
                f.ts("dve", S_["kd"], ktok, EK[:, m, cd:cd + 1], None, op0=ALU.mult)
                yield
                rows = bk[:, 256:512]
                f.mm(rows, ones, E.rearrange("p a b -> p (a b)"))
                yield
                rows3 = rows.rearrange("p (a b) -> p a b", a=2)
                f.stt("dve", E, rows3, gc, NM[d], ALU.subtract, ALU.min)
                f.act(B["eg"], rows3[:, 0, :], AF.Exp)
                yield
                f.act(E, E, AF.Exp)
                f.tt("pool", S_["qd"], B["eg"], qT, ALU.mult)
                yield
                W = B["W"]
                NT_ = W[:, 1, :]
                f.tt("dve", NT_, kqkk[:, 1, :], E[:, 1, :], ALU.mult)
                f.tt("dve", S_["AT"], kqkk[:, 0, :], E[:, 0, :], ALU.mult)
                yield
                f.tr(slot[0], NT_, ident)
                f.tt("pool", W[:, 0, :], ident, NT_, ALU.subtract)
                yield
                f.copy("act", W[:, 2, :], slot[0])
                yield
                f.mm(slot[1], W[:, 2, :], W[:, 1, :])
                f.mm(slot[2], W[:, 1, :], W[:, 2, :])
                yield
                f.copy("act", W[:, 1:3, :], bk[:, 128:384].rearrange("p (a b) -> p a b", a=2))
                yield
                for lev in range(1, 5):
                    f.mm(bk[:, 0:256], W[:, 2, :], W[:, 0:2, :].rearrange("p a b -> p (a b)"))
                    f.mm(slot[2], W[:, 1, :], W[:, 2, :])
                    yield
                    f.tt("dve", W[:, 0, :], W[:, 0, :], slot[0], ALU.add)
                    f.copy("act", W[:, 1:3, :], bk[:, 128:384].rearrange("p (a b) -> p a b", a=2))
                    yield
                f.mm(slot[0], W[:, 2, :], W[:, 0, :])
                yield
                f.tt("dve", W[:, 0, :], W[:, 0, :], slot[0], ALU.add)
                yield
                Tt = W[:, 0, :]
                f.mm(slot[1], Tt, B["R0"][:, 0:128])
                f.mm(slot[2], B["R0"][:, 128:256], Tt)
                yield
                f.copy("act", S_["u"], slot[1])
                f.copy("dve", S_["wT"], slot[2])
                yield

            def init_state(d, h, idx):
                dst = Sst[d][idx]
                if u == 1:
                    f.dma("sp", dst, s0_d[:, d, h, :], "s0%d" % d)
                else:
                    f.copy("pool", dst, zeroS)
                f.copy("act", Sbf[d][idx], dst)

            Sidx = [0, 0]

            def scan_chain(h, d, tiles, Bs):
                cd = d * 8 + h
                bS = bank(2 + d)
                vnp = bS[:, 0:128]
                otp = bS[:, 128:192]
                for m, B in zip(tiles, Bs):
                    for ci in range(2):
                        c = ci if d == 0 else 1 - ci
                        lo, hi = c * 64, (c + 1) * 64
                        if u == 0:
                            first = (m % 2 == 0 and c == 0) if d == 0 else (m % 2 == 1 and c == 1)
                            if first:
                                init_state(d, h, Sidx[d])
                        Scur, Snew = Sst[d][Sidx[d]], Sst[d][1 - Sidx[d]]
                        Sbc, Sbn = Sbf[d][Sidx[d]], Sbf[d][1 - Sidx[d]]
                        f.mm(vnp, B["wT"], Sbc)
                        yield
                        f.tt("dve", B["vn"][lo:hi, :], B["u"][lo:hi, :], vnp[lo:hi, :], ALU.subtract)
                        yield
                        f.mm(otp, Sbc, B["qd"][:, lo:hi], start=True, stop=False)
                        f.mm(otp, B["vn"][lo:hi, :], B["AT"][lo:hi, lo:hi], start=False, stop=True)
                        f.mm(vnp, B["kd"][lo:hi, :], B["vn"][lo:hi, :])
                        yield
                        t0 = m * 128 + lo
                        f.stt("dve", Snew, Scur, GLB[:, m, c, cd:cd + 1], vnp, ALU.mult, ALU.add)
                        osum = osums[h % 2]
                        f.tt("dve", osum[:, t0:t0 + 64], osum[:, t0:t0 + 64], otp, ALU.add)
                        yield
                        f.copy("act", Sbn, Snew)
                        yield
                        Sidx[d] = 1 - Sidx[d]
                        if u == 0:
                            last = (m % 2 == 1 and c == 1) if d == 0 else (m % 2 == 0 and c == 0)
                            if last:
                                seq = m // 2
                                f.copy("pool", sout[d], Sst[d][Sidx[d]])
                                dst = (nsf_d if d == 0 else nsb_d)[seq, h, :, :]
                                f.dma("sp", dst, sout[d], "so%d" % d)

            def finalize(h):
                osum = osums[h % 2]
                f.act(rnf, osum, AF.Square)
                yield
                for half in range(2):
                    f.mm(bank(half), ones, rnf[:, half * 512:(half + 1) * 512])
                    yield
                    f.act(rnf[:, half * 512:(half + 1) * 512], bank(half), AF.Ln, bias=epsT, scale=1.0 / 128)
                    yield
                f.act(rnf, rnf, AF.Exp, scale=-0.5)
                yield
                f.stt("dve", rnf, osum, gnw, rnf, ALU.mult, ALU.mult)
                yield
                f.tt("dve", oT[:, h, :], rnf, oT[:, h, :], ALU.mult)
                if u == 0 and h == 0:
                    dump("os0", osum[:, 0:256])
                yield

            def run_rr(chains):
                act_ = list(chains)
                while act_:
                    for g in list(act_):
                        try:
                            next(g)
                        except StopIteration:
                            act_.remove(g)

            def limited(g, n):
                for _ in range(n):
                    try:
                        next(g)
                    except StopIteration:
                        return
                    yield

            def tiles_of(r):
                return [(0, 2 * r), (0, 2 * r + 1), (1, 7 - 2 * r), (1, 6 - 2 * r)]

            g0 = proj_gen(0, QKV[0])
            for _ in g0:
                pass
            gen_next = None
            fin_pending = None
            NR = 4
            for R in range(H * NR + 1):
                chains = []
                if R < H * NR:
                    h, r = divmod(R, NR)
                    if r == 0:
                        if u == 0 and h == 0:
                            dump("q0", QKV[0][0][:, 0:256])
                            dump("k0", QKV[0][1][:, 0:256])
                            dump("v0", QKV[0][2][:, 0:256])
                        if h + 1 < H:
                            gen_next = proj_gen(h + 1, QKV[(h + 1) % 2])
                    for ci, (d, m) in enumerate(tiles_of(r)):
                        chains.append(prep(h, m, d, QKV[h % 2], PB[ci], SB[R % 2][ci], ci))
                    if gen_next is not None:
                        chains.append(gen_next if r == NR - 1 else limited(gen_next, 20))
                if R >= 1:
                    h2, r2 = divmod(R - 1, NR)
                    if r2 == 0:
                        f.memset("pool", osums[h2 % 2], 0.0)
                        for d in range(2):
                            Sidx[d] = 0
                            if u == 1:
                                init_state(d, h2, 0)
                    tl = tiles_of(r2)
                    Bp = SB[(R - 1) % 2]
                    chains.append(scan_chain(h2, 0, [tl[0][1], tl[1][1]], [Bp[0], Bp[1]]))
                    chains.append(scan_chain(h2, 1, [tl[2][1], tl[3][1]], [Bp[2], Bp[3]]))
                if fin_pending is not None:
                    chains.append(fin_pending)
                    fin_pending = None
                run_rr(chains)
                if R >= 1 and (R - 1) % NR == NR - 1:
                    fin_pending = finalize((R - 1) // NR)
            for _ in fin_pending:
                pass

        phase_A()
        dump("modc", MODC)
        if stop_after != "A":
            for u in units:
                unit(u)
        f.emit()
        nc._arena_peak = (AP_.peak, A_B.peak, A_S.peak, A_O.peak)
    return nc


_PROG = {}


def _col(v, n):
    return np.ascontiguousarray(np.asarray(v, np.float32).reshape(n, 128).T)


def prep_inputs(inp, cores=range(8)):
    f32 = lambda a: np.asarray(a, np.float32)
    w_in = f32(inp["w_in"])[0]
    perm = list(range(5152, 6176)) + list(range(4096, 4128))
    for h in range(H):
        for t in range(4):
            perm += list(range(t * 1024 + h * 128, t * 1024 + (h + 1) * 128))
    perm += list(range(4128, 5152))
    w_in_r = np.ascontiguousarray(w_in[:, perm])
    permu = []
    for j in range(NJ):
        permu += list(range(j * 128, (j + 1) * 128)) + list(range(DFF + j * 128, DFF + (j + 1) * 128))
    w_up_r = np.ascontiguousarray(f32(inp["w_up"])[0][:, permu])
    fcw = f32(inp["ffn_conv_w"])[0].reshape(9, 2 * DFF)[:, permu]
    fcw = np.ascontiguousarray(fcw.reshape(9, 88, 128).transpose(2, 1, 0))
    fcb = np.ascontiguousarray(f32(inp["ffn_conv_b"])[0][permu].reshape(88, 128).T)
    caw = f32(inp["conv_a_w"])[0, 0]
    caw = np.ascontiguousarray(caw.reshape(5, 3, 8, 128).transpose(3, 2, 1, 0))
    c0, c1 = _make_consts()
    shared = dict(
        cst0=c0, cst1=c1,
        w_ada=np.ascontiguousarray(f32(inp["w_ada"])[0]),
        b_ada=np.ascontiguousarray(f32(inp["b_ada"])[0][None, :]),
        w_in=w_in_r,
        w_out=np.ascontiguousarray(f32(inp["w_out"])[0]),
        w_up=w_up_r,
        w_down=np.ascontiguousarray(f32(inp["w_down"])[0]),
        sgw=np.ascontiguousarray(f32(inp["sgu_w"])[0].transpose(2, 0, 1)),
    )
    row = np.zeros((1, NROW), np.float32)
    row[0, RW_DTB:RW_DTB + 16] = f32(inp["dt_bias"])[0].reshape(16)
    row[0, RW_ALOG:RW_ALOG + 16] = f32(inp["a_log"])[0].reshape(16)
    row[0, RW_SNW:RW_SNW + 1024] = f32(inp["sgu_norm_w"])[0]
    row[0, RW_SGB:RW_SGB + 1024] = f32(inp["sgu_b"])[0].reshape(1024)
    shared["rowp"] = row
    prm0 = np.zeros((128, NPRM), np.float32)
    prm0[:, PR_N1:PR_N1 + 16] = _col(inp["norm1_w"][0], 16)
    prm0[:, PR_N2:PR_N2 + 16] = _col(inp["norm2_w"][0], 16)
    prm0[:, PR_FN:PR_FN + 16] = _col(inp["final_norm_w"], 16)
    prm0[:, PR_GN] = f32(inp["gdn_norm_w"])[0]
    prm0[:, PR_CA:PR_CA + 120] = caw.reshape(128, 120)
    prm0[:, PR_FW:PR_FW + 792] = fcw.reshape(128, 792)
    prm0[:, PR_FB:PR_FB + 88] = fcb
    xp = f32(inp["x_prompt"])
    xs = f32(inp["x_sample"])
    sf = f32(inp["state_fwd"])
    sb = f32(inp["state_bwd"])
    cc = f32(inp["c"])
    cctx = f32(inp["c_ctx"])
    maps = []
    for i in cores:
        m = dict(shared)
        xc = np.stack([xp[4 * i:4 * i + 4].reshape(T, D), xs[i]], axis=0)
        m["xT"] = np.ascontiguousarray(xc.transpose(0, 2, 1))
        prm = prm0.copy()
        prm[:, PR_C:PR_C + 16] = _col(cctx, 16)
        prm[:, PR_C + 16:PR_C + 32] = _col(cc[i], 16)
        m["prm"] = prm
        s0 = np.stack([sf[i, 0], sb[i, 0]], axis=0)
        m["s0"] = np.ascontiguousarray(s0.transpose(2, 0, 1, 3))
        maps.append(m)
    return maps


def kernel(**inputs):
    if "nc" not in _PROG:
        _PROG["nc"] = build_program()
    nc = _PROG["nc"]
    maps = prep_inputs(inputs)
    res = run_bass_kernel_spmd(nc, maps, core_ids=list(range(8)))
    r = res.results
    y_prompt = np.concatenate([r[i]["y"][0].reshape(4, 256, D) for i in range(8)], axis=0).astype(np.float32)
    y_sample = np.stack([r[i]["y"][1] for i in range(8)], axis=0).astype(np.float32)
    nsf = np.concatenate([r[i]["nsf"][:, None] for i in range(8)], axis=0).astype(np.float32)
    nsb = np.concatenate([r[i]["nsb"][:, None] for i in range(8)], axis=0).astype(np.float32)
    return (y_prompt, y_sample, nsf, nsb)
```

```python
import contextlib
import numpy as np
import concourse.bass as bass
import concourse.mybir as mybir
from concourse.bass_utils import run_bass_kernel_spmd

F32 = mybir.dt.float32
BF16 = mybir.dt.bfloat16
AF = mybir.ActivationFunctionType
ALU = mybir.AluOpType
AX = mybir.AxisListType
_DT_SIZE = {F32: 4, BF16: 2, mybir.dt.int32: 4}


class Op:
    __slots__ = ("eng", "fn", "deps", "dma_key", "dma_val", "milestone", "ms_no")

    def __init__(self, eng, fn, dma_key=None):
        self.eng = eng
        self.fn = fn
        self.deps = set()
        self.dma_key = dma_key
        self.dma_val = 0
        self.milestone = False
        self.ms_no = 0


def _region(ap):
    t = ap.tensor
    aps = ap.ap
    off = int(ap.offset)
    dsz = _DT_SIZE.get(ap.dtype, 4)
    sp = str(ap.space)
    if sp in ("SB", "PSUM"):
        pstep = aps[0][0]
        if pstep == 0:
            row = 1
            for s in list(t.shape)[1:]:
                row *= int(s)
            pstep = row * _DT_SIZE.get(t.dtype, 4) // dsz
        plo = off // pstep
        flo = off % pstep
        phi = plo + aps[0][1]
        span = 1
        for st, cnt in aps[1:]:
            span += abs(st) * (cnt - 1)
        return (t.name, plo, phi, flo * dsz, (flo + span) * dsz)
    span = 1
    for st, cnt in aps:
        span += abs(st) * (cnt - 1)
    return (t.name, 0, 1, off * dsz, (off + span) * dsz)


class Rec:
    __slots__ = ("plo", "phi", "lo", "hi", "writer", "readers")

    def __init__(self, plo, phi, lo, hi, writer):
        self.plo, self.phi, self.lo, self.hi = plo, phi, lo, hi
        self.writer = writer
        self.readers = []


class Fw:
    ENGS = ("pe", "act", "dve", "pool", "sp")

    def __init__(self, nc):
        self.nc = nc
        self.q = {e: [] for e in self.ENGS}
        self.regions = {}
        self.untracked = set()
        self.dma_cnt = {}
        self.psum_last = {}

    def _track(self, ap, op, is_write):
        name, plo, phi, lo, hi = _region(ap)
        if name in self.untracked:
            return
        if str(ap.space) == "PSUM":
            for b in range(lo // 2048, (hi - 1) // 2048 + 1):
                d = self.psum_last.setdefault(b, {})
                for e2, o2 in d.items():
                    if e2 != op.eng:
                        op.deps.add(o2)
                d[op.eng] = op
        recs = self.regions.setdefault(name, [])
        deps = op.deps
        if is_write:
            keep = []
            for r in recs:
                if r.plo < phi and plo < r.phi and r.lo < hi and lo < r.hi:
                    if r.writer is not None:
                        deps.add(r.writer)
                    for rd, a_, b_, c_, d_ in r.readers:
                        if a_ < phi and plo < b_ and c_ < hi and lo < d_:
                            deps.add(rd)
                    if plo <= r.plo and r.phi <= phi and lo <= r.lo and r.hi <= hi:
                        continue
                    keep.append(r)
                else:
                    keep.append(r)
            keep.append(Rec(plo, phi, lo, hi, op))
            self.regions[name] = keep
        else:
            cover = False
            for r in recs:
                if r.plo < phi and plo < r.phi and r.lo < hi and lo < r.hi:
                    if r.writer is not None:
                        deps.add(r.writer)
                    r.readers.append((op, plo, phi, lo, hi))
                    if r.plo <= plo and phi <= r.phi and r.lo <= lo and hi <= r.hi:
                        cover = True
            if not cover:
                r = Rec(plo, phi, lo, hi, None)
                r.readers.append((op, plo, phi, lo, hi))
                recs.append(r)

    def op(self, eng, fn, reads=(), writes=(), dma_key=None):
        o = Op(eng, fn, dma_key)
        for ap in reads:
            if ap is not None and not isinstance(ap, (int, float)):
                self._track(ap, o, False)
        for ap in writes:
            self._track(ap, o, True)
        o.deps.discard(o)
        if dma_key is not None:
            c = self.dma_cnt.get(dma_key, 0) + 16
            self.dma_cnt[dma_key] = c
            o.dma_val = c
        self.q[eng].append(o)
        return o

    def mm(self, out, lhsT, rhs, start=True, stop=True):
        return self.op("pe", lambda e: e.matmul(out, lhsT, rhs, start=start, stop=stop),
                       reads=[lhsT, rhs], writes=[out])

    def tr(self, out, in_, ident):
        return self.op("pe", lambda e: e.transpose(out, in_, ident), reads=[in_, ident], writes=[out])

    def act(self, out, in_, func, bias=None, scale=None, accum_out=None):
        kw = {}
        rd = [in_]
        if bias is not None:
            kw["bias"] = bias
            rd.append(bias)
        if scale is not None:
            kw["scale"] = scale
            rd.append(scale)
        wr = [out]
        if accum_out is not None:
            kw["accum_out"] = accum_out
            wr.append(accum_out)
        return self.op("act", lambda e: e.activation(out, in_, func, **kw), reads=rd, writes=wr)

    def tt(self, eng, out, in0, in1, op):
        return self.op(eng, lambda e: e.tensor_tensor(out, in0, in1, op), reads=[in0, in1], writes=[out])

    def ts(self, eng, out, in0, s1, s2=None, op0=ALU.mult, op1=None):
        kw = {}
        if op1 is not None:
            kw["op1"] = op1
        return self.op(eng, lambda e: e.tensor_scalar(out, in0, s1, s2, op0, **kw),
                       reads=[in0, s1, s2], writes=[out])

    def stt(self, eng, out, in0, scalar, in1, op0, op1):
        return self.op(eng, lambda e: e.scalar_tensor_tensor(out, in0, scalar, in1, op0, op1),
                       reads=[in0, scalar, in1], writes=[out])

    def copy(self, eng, out, in_):
        if eng == "act":
            return self.op("act", lambda e: e.copy(out, in_), reads=[in_], writes=[out])
        return self.op(eng, lambda e: e.tensor_copy(out, in_), reads=[in_], writes=[out])

    def memset(self, eng, ap, val):
        return self.op(eng, lambda e: e.memset(ap, val), writes=[ap])

    def recip(self, out, in_):
        return self.op("dve", lambda e: e.reciprocal(out, in_), reads=[in_], writes=[out])

    def reduce(self, eng, out, in_, op=ALU.add, axis=AX.X):
        return self.op(eng, lambda e: e.tensor_reduce(out, in_, axis, op), reads=[in_], writes=[out])

    def dma(self, queue, out, in_, key):
        return self.op(queue, lambda e: e.dma_start(out, in_), reads=[in_], writes=[out], dma_key=key)

    def emit(self):
        nc = self.nc
        for e in self.ENGS:
            for o in self.q[e]:
                for d in o.deps:
                    if d.dma_key is None and not (d.eng == "pe" and o.eng == "pe"):
                        d.milestone = True
        for e in self.ENGS:
            n = 0
            for o in self.q[e]:
                if o.milestone:
                    n += 1
                    o.ms_no = n
        with contextlib.ExitStack() as st:
            esem = {e: st.enter_context(nc.semaphore("s_" + e)) for e in self.ENGS}
            dsem = {k: st.enter_context(nc.semaphore("d_%s" % (k,))) for k in self.dma_cnt}
            block = st.enter_context(nc.Block())
            fw = self

            def run(engname, eng):
                waited = {}
                for o in fw.q[engname]:
                    need = {}
                    for d in o.deps:
                        if d.dma_key is not None:
                            s, v = ("d", d.dma_key), d.dma_val
                        else:
                            if d.eng == "pe" and engname == "pe":
                                continue
                            s, v = ("e", d.eng), d.ms_no
                        if waited.get(s, 0) >= v:
                            continue
                        if need.get(s, 0) < v:
                            need[s] = v
                    for s, v in need.items():
                        eng.wait_ge(dsem[s[1]] if s[0] == "d" else esem[s[1]], v)
                        waited[s] = v
                    ins = o.fn(eng)
                    if o.dma_key is not None:
                        ins.then_inc(dsem[o.dma_key], 16)
                    elif o.milestone:
                        ins.then_inc(esem[engname], 1)
                if engname == "sp":
                    for k, v in fw.dma_cnt.items():
                        if waited.get(("d", k), 0) < v:
                            eng.wait_ge(dsem[k], v)

            @block.tensor
            def _(eng):
                run("pe", eng)

            @block.scalar
            def _(eng):
                run("act", eng)

            @block.vector
            def _(eng):
                run("dve", eng)

            @block.gpsimd
            def _(eng):
                run("pool", eng)

            @block.sync
            def _(eng):
                run("sp", eng)


class Arena:
    def __init__(self, t, words, base=0):
        self.t = t
        self.words = words
        self.base = base
        self.top = 0
        self.peak = 0

    def alloc(self, shape, dt=F32):
        shape = [int(s) for s in (shape if isinstance(shape, (list, tuple)) else [shape])]
        n = int(np.prod(shape))
        nbytes = n * (2 if dt == BF16 else 4)
        w = (nbytes + 31) // 32 * 8
        off = self.base + self.top
        self.top += w
        self.peak = max(self.peak, self.top)
        assert self.top <= self.words, "arena overflow: %d > %d words" % (self.top, self.words)
        v = self.t[:, off:off + (nbytes + 3) // 4]
        if dt == BF16:
            v = v.bitcast(BF16)
        if len(shape) > 1:
            names = ["a%d" % i for i in range(len(shape))]
            pat = "p (%s) -> p %s" % (" ".join(names), " ".join(names))
            v = v.rearrange(pat, **{names[i]: shape[i] for i in range(1, len(shape))})
        return v

    def mark(self):
        return self.top

    def release(self, m):
        self.top = m


class WStream:
    def __init__(self, f, slots, name, queue="pool", depth=1):
        self.f, self.slots, self.name, self.queue, self.depth = f, slots, name, queue, depth
        self.loads = []
        self.issued = 0
        self.next = 0

    def plan(self, src, view=None):
        self.loads.append((src, view))

    def ahead(self, n=1):
        tgt = min(self.next - 1 + n, len(self.loads) - 1)
        ns = len(self.slots)
        while self.issued <= tgt:
            j = self.issued
            src, view = self.loads[j]
            slot = self.slots[j % ns]
            self.f.dma(self.queue, view(slot) if view else slot, src, "%s%d" % (self.name, j % ns))
            self.issued += 1

    def get(self, depth=None):
        i = self.next
        self.next += 1
        n = len(self.slots)
        depth = self.depth if depth is None else depth
        while self.issued <= min(i + depth, len(self.loads) - 1):
            j = self.issued
            src, view = self.loads[j]
            slot = self.slots[j % n]
            self.f.dma(self.queue, view(slot) if view else slot, src, "%s%d" % (self.name, j % n))
            self.issued += 1
        src, view = self.loads[i]
        slot = self.slots[i % n]
        return view(slot) if view else slot


D = 2048
KC = 16
T = 1024
NT = 8
H = 8
P_IN = 6176
DFF = 5632
NJ = 44
NG = 4
JG = NJ // NG
SV0, AB0, HD0, U0 = 0, 1024, 1056, 5152
EPS = 1e-6
NEG = -30000.0

C_ID, C_ONE = 0, 128
NCST0 = 256
C_TRIF, C_TRIB, C_BLK, C_CH0, C_CH1, C_NMF, C_NMB = 0, 128, 256, 384, 512, 640, 896
NCST1 = 1152
PR_C, PR_N1, PR_N2, PR_FN, PR_GN, PR_CA, PR_FW, PR_FB = 0, 32, 48, 64, 80, 81, 201, 993
NPRM = 1081
RW_DTB, RW_ALOG, RW_SNW, RW_SGB = 0, 16, 32, 1056
NROW = 2080


def _make_consts():
    c0 = np.zeros((128, NCST0), np.float32)
    c0[:, C_ID:C_ID + 128] = np.eye(128)
    c0[:, C_ONE:C_ONE + 128] = 1.0
    c = np.zeros((128, NCST1), np.float32)
    idx = np.arange(128)
    same = (idx[:, None] // 64) == (idx[None, :] // 64)
    c[:, C_TRIF:C_TRIF + 128] = (same & (idx[:, None] <= idx[None, :]))
    c[:, C_TRIB:C_TRIB + 128] = (same & (idx[:, None] >= idx[None, :]))
    c[:, C_BLK:C_BLK + 128] = same
    c[:, C_CH0:C_CH0 + 128] = (idx[:, None] < 64)
    c[:, C_CH1:C_CH1 + 128] = (idx[:, None] >= 64)
    nmf = np.full((128, 2, 128), NEG, np.float32)
    nmf[:, 0][same & (idx[None, :] >= idx[:, None])] = 0.0
    nmf[:, 1][same & (idx[None, :] > idx[:, None])] = 0.0
    nmb = np.full((128, 2, 128), NEG, np.float32)
    nmb[:, 0][same & (idx[None, :] <= idx[:, None])] = 0.0
    nmb[:, 1][same & (idx[None, :] < idx[:, None])] = 0.0
    c[:, C_NMF:C_NMF + 256] = nmf.reshape(128, 256)
    c[:, C_NMB:C_NMB + 256] = nmb.reshape(128, 256)
    return c0, c


def build_program(stop_after=None, dumps=(), units=(0, 1)):
    nc = bass.Bass("TRN2", target_bir_lowering=False)
    din = lambda n, s: nc.dram_tensor(n, list(s), F32, kind="ExternalInput").ap()
    dout = lambda n, s: nc.dram_tensor(n, list(s), F32, kind="ExternalOutput").ap()
    xT_d = din("xT", [2, D, T])
    cst0_d = din("cst0", [128, NCST0])
    cst1_d = din("cst1", [128, NCST1])
    prm_d = din("prm", [128, NPRM])
    row_d = din("rowp", [1, NROW])
    s0_d = din("s0", [128, 2, H, 128])
    sgw_d = din("sgw", [128, 8, 128])
    w_ada_d = din("w_ada", [D, 6 * D])
    b_ada_d = din("b_ada", [1, 6 * D])
    w_in_d = din("w_in", [D, P_IN])
    w_out_d = din("w_out", [D, D])
    w_up_d = din("w_up", [D, 2 * DFF])
    w_dn_d = din("w_down", [DFF, D])
    y_d = dout("y", [2, T, D])
    nsf_d = dout("nsf", [4, H, 128, 128])
    nsb_d = dout("nsb", [4, H, 128, 128])
    dump_d = {n: dout("dbg_" + n, s) for n, s in dumps}

    f = Fw(nc)
    f.untracked |= {"xT", "cst0", "cst1", "prm", "rowp", "s0", "sgw", "w_ada", "b_ada", "w_in", "w_out", "w_up", "w_down"}
    dbgctr = [0]

    with contextlib.ExitStack() as st:
        ARW = 52000
        arena_t = st.enter_context(nc.sbuf_tensor("arena", [128, ARW], F32))
        ps = st.enter_context(nc.psum_tensor("ps", [128, 8, 512], F32))

        def bank(b, n=1):
            if n == 1:
                return ps[:, b, :]
            return ps[:, b:b + n, :].rearrange("p a b -> p (a b)")

        W_HT, W_OT, W_XR, W_SC = 8192, 8192, 16384, 8704
        W_P = ARW - (W_HT + W_OT + W_XR + W_SC)
        AP_ = Arena(arena_t, W_P, 0)
        o_ht = W_P
        o_ot = o_ht + W_HT
        o_xr = o_ot + W_OT
        o_sc = o_xr + W_XR
        hT = arena_t[:, o_ht:o_ht + W_HT].bitcast(BF16).rearrange("p (c t) -> p c t", c=16)
        oT = arena_t[:, o_ot:o_ot + W_OT].bitcast(BF16).rearrange("p (c t) -> p c t", c=16)
        XR = arena_t[:, o_xr:o_xr + W_XR].rearrange("p (c t) -> p c t", c=16)
        A_O = Arena(arena_t, W_OT, o_ot)
        A_B = Arena(arena_t, W_XR + W_SC, o_xr)
        A_S = Arena(arena_t, W_SC, o_sc)

        cst0 = AP_.alloc([NCST0])
        prm = AP_.alloc([NPRM])
        identb = AP_.alloc([128], BF16)
        onesb = AP_.alloc([128], BF16)
        epsT = AP_.alloc([1])
        MODC = AP_.alloc([2, 6, 16])
        GCOL = AP_.alloc([2, 2, 16])
        f.dma("sp", cst0, cst0_d, "cst0")
        f.dma("sp", prm, prm_d, "prm")
        f.memset("dve", epsT, EPS)
        ident = cst0[:, C_ID:C_ID + 128]
        ones = cst0[:, C_ONE:C_ONE + 128]
        f.copy("dve", identb, ident)
        f.copy("dve", onesb, ones)

        def dump(name, ap_sb):
            if name in dump_d:
                dbgctr[0] += 1
                f.dma("pool" if ap_sb.dtype == BF16 else "sp", dump_d[name], ap_sb, "dbg%d" % dbgctr[0])

        NWS = 2
        wslot = [AP_.alloc([16, 512], BF16) for _ in range(NWS)]
        WS = WStream(f, wslot, "ws")
        wsrc = lambda ap: ap.rearrange("(c p) n -> p c n", p=128)
        for blk in range(24):
            WS.plan(wsrc(w_ada_d[:, blk * 512:(blk + 1) * 512]))
        for u_ in units:
            WS.plan(wsrc(w_in_d[:, SV0:SV0 + 512]))
            WS.plan(wsrc(w_in_d[:, SV0 + 512:SV0 + 1024]))
            for g_ in range(2):
                WS.plan(wsrc(w_in_d[:, U0 + g_ * 512:U0 + (g_ + 1) * 512]))
            for h_ in range(H):
                WS.plan(wsrc(w_in_d[:, HD0 + h_ * 512:HD0 + (h_ + 1) * 512]))
            for c_ in range(4):
                WS.plan(wsrc(w_out_d[:, c_ * 512:(c_ + 1) * 512]))
            for j_ in range(NJ // 2):
                WS.plan(wsrc(w_up_d[:, j_ * 512:(j_ + 1) * 512]))
        WD = None

        def phase_A():
            A_B.release(0)
            sc = A_B.alloc([2, 16])
            sbc = A_B.alloc([2, 16, 128], BF16)
            f.act(sc, prm[:, PR_C:PR_C + 32].rearrange("p (u c) -> p u c", u=2), AF.Silu)
            f.copy("dve", sbc, sc.unsqueeze(3).broadcast_to([128, 2, 16, 128]))
            bb = [A_B.alloc([2048]) for _ in range(2)]
            tmpm = [A_B.alloc([2048]) for _ in range(2)]
            scr = A_B.alloc([16, 128])
            for s in range(6):
                f.dma("sp", bb[s % 2], b_ada_d[:, s * D:(s + 1) * D].partition_broadcast(128), "bb%d" % (s % 2))
                for cb in range(4):
                    blk = s * 4 + cb
                    wa = WS.get()
                    for u in range(2):
                        pb = bank((blk * 2 + u) % 8)
                        for kk in range(KC):
                            f.mm(pb, sbc[:, u, kk, :], wa[:, kk, :], start=(kk == 0), stop=(kk == KC - 1))
                        f.tt("dve", tmpm[u][:, cb * 512:(cb + 1) * 512], pb, bb[s % 2][:, cb * 512:(cb + 1) * 512], ALU.add)
                for u in range(2):
                    f.tt("dve", scr, tmpm[u].rearrange("p (c j) -> p c j", c=16),
                         ident.unsqueeze(1).broadcast_to([128, 16, 128]), ALU.mult)
                    f.reduce("dve", MODC[:, u, s, :], scr)
            for u in range(2):
                f.stt("dve", GCOL[:, u, 0, :], MODC[:, u, 1, :], 1.0, prm[:, PR_N1:PR_N1 + 16], ALU.add, ALU.mult)
                f.stt("dve", GCOL[:, u, 1, :], MODC[:, u, 4, :], 1.0, prm[:, PR_N2:PR_N2 + 16], ALU.add, ALU.mult)

        def rstd_rows(src, AR, load_fn=None):
            rr = AR.alloc([1024])
            sq = [AR.alloc([1024], BF16) for _ in range(2)]
            for c in range(KC):
                if load_fn is not None:
                    load_fn(c)
                f.act(sq[c % 2], src[:, c, :], AF.Square)
                for half in range(2):
                    f.mm(bank(half), onesb, sq[c % 2][:, half * 512:(half + 1) * 512], start=(c == 0), stop=(c == KC - 1))
            f.act(rr, bank(0, 2), AF.Ln, bias=epsT, scale=1.0 / D)
            f.act(rr, rr, AF.Exp, scale=-0.5)
            return rr

        def modulate_to(u, which, src, rr, AR, dst):
            tmp = [AR.alloc([1024]) for _ in range(2)]
            sec_shift = 0 if which == 0 else 3
            for c in range(KC):
                t = tmp[c % 2]
                f.stt("dve", t, src[:, c, :], GCOL[:, u, which, c:c + 1], rr, ALU.mult, ALU.mult)
                f.act(dst[:, c, :], t, AF.Identity, bias=MODC[:, u, sec_shift, c:c + 1])

        def unit(u):
            nseq, L = (4, 256) if u == 0 else (1, 1024)

            A_S.release(0)

            def load_x(c):
                f.dma("sp", XR[:, c, :], xT_d[u, c * 128:(c + 1) * 128, :], "xT%d" % c)
            rr = rstd_rows(XR, A_S, load_x)
            modulate_to(u, 0, XR, rr, A_S, hT)
            if u == 0:
                dump("hT", hT[:, :, 0:256])
            if stop_after == "B":
                return

            A_B.release(0)
            cst1 = A_B.alloc([NCST1])
            f.dma("sp", cst1, cst1_d, "cst1")
            GA = A_B.alloc([8, 16, 2])
            BET = A_B.alloc([8, 16])
            BEG = A_B.alloc([8, 16])
            EK = A_B.alloc([8, 16])
            GLB = A_B.alloc([8, 2, 16])
            mC1 = A_B.mark()
            svn = A_B.alloc([8, 1024], BF16)
            rowb = A_B.alloc([32])
            snw_b = A_B.alloc([1024])
            f.dma("sp", rowb, row_d[:, 0:32].partition_broadcast(128), "rwa")
            f.dma("sp", snw_b, row_d[:, RW_SNW:RW_SNW + 1024].partition_broadcast(128), "rwb")
            negA = A_B.alloc([16])
            f.act(negA, rowb[:, 16:32], AF.Exp)
            f.ts("dve", negA, negA, -1.0, None, op0=ALU.mult)
            wab = A_B.alloc([16, 32], BF16)
            f.dma("pool", wab, w_in_d[:, AB0:AB0 + 32].rearrange("(c p) n -> p c n", p=128), "wab")
            ws0 = WS.get(depth=1)
            ws1 = WS.get(depth=0)
            ABR = A_B.alloc([8, 32])
            gsv = [A_B.alloc([1024]) for _ in range(2)]
            junk = A_B.alloc([1024], BF16)
            ss2 = A_B.alloc([8])
            rs2 = A_B.alloc([8])
            for m in range(NT):
                b0 = (m % 2) * 3
                for kk in range(KC):
                    lt = hT[:, kk, m * 128:(m + 1) * 128]
                    f.mm(bank(b0), lt, ws0[:, kk, :], start=(kk == 0), stop=(kk == KC - 1))
                    f.mm(bank(b0 + 1), lt, ws1[:, kk, :], start=(kk == 0), stop=(kk == KC - 1))
                    f.mm(bank(b0 + 2)[:, 0:32], lt, wab[:, kk, :], start=(kk == 0), stop=(kk == KC - 1))
                g = gsv[m % 2]
                f.act(g, bank(b0, 2), AF.Gelu_apprx_tanh)
                f.act(junk, g, AF.Square, accum_out=ss2[:, m:m + 1])
                f.copy("dve", ABR[:, m, :], bank(b0 + 2)[:, 0:32])
                f.act(rs2[:, m:m + 1], ss2[:, m:m + 1], AF.Sqrt, bias=epsT, scale=1.0 / 1024)
                f.recip(rs2[:, m:m + 1], rs2[:, m:m + 1])
                f.stt("dve", svn[:, m, :], g, rs2[:, m:m + 1], snw_b, ALU.mult, ALU.mult)
            WS.ahead(1)
            zt = A_B.alloc([8, 16])
            GL = A_B.alloc([8, 16])
            LNB = A_B.alloc([8, 16])
            EGt = A_B.alloc([8, 16])
            f.tt("dve", zt, ABR[:, :, 0:16], rowb[:, 0:16].unsqueeze(1).broadcast_to([128, 8, 16]), ALU.add)
            f.act(zt, zt, AF.Exp)
            f.act(zt, zt, AF.Ln, bias=1.0)
            f.tt("dve", GL, zt, negA.unsqueeze(1).broadcast_to([128, 8, 16]), ALU.mult)
            f.act(BET, ABR[:, :, 16:32], AF.Sigmoid)
            f.act(LNB, BET, AF.Ln)
            pc = bank(7).rearrange("p (m c) -> p m c", m=8)
            for m in range(NT):
                f.mm(pc[:, m, 0:8], cst1[:, C_TRIF:C_TRIF + 128], GL[:, m, 0:8])
                f.mm(pc[:, m, 8:16], cst1[:, C_TRIB:C_TRIB + 128], GL[:, m, 8:16])
                f.mm(pc[:, m, 16:32], cst1[:, C_BLK:C_BLK + 128], GL[:, m, :])
                f.mm(pc[:, m, 32:48], cst1[:, C_CH0:C_CH0 + 128], GL[:, m, :])
                f.mm(pc[:, m, 48:64], cst1[:, C_CH1:C_CH1 + 128], GL[:, m, :])
            f.copy("dve", GA[:, :, :, 0], pc[:, :, 0:16])
            f.tt("dve", GA[:, :, :, 1], pc[:, :, 0:16], LNB, ALU.add)
            f.act(EGt, pc[:, :, 0:16], AF.Exp)
            f.tt("dve", BEG, EGt, BET, ALU.mult)
            f.tt("dve", zt, pc[:, :, 16:32], GA[:, :, :, 0], ALU.subtract)
            f.act(EK, zt, AF.Exp)
            f.act(GLB, pc[:, :, 32:64].rearrange("p m (a c) -> p m a c", a=2), AF.Exp)
            if u == 0:
                dump("svn", svn[:, 0:2, :])
                dump("GL", GL)
                dump("BET", BET)
            if stop_after == "C1":
                return

            sgwT = A_B.alloc([8, 128], BF16)
            f.dma("pool", sgwT, sgw_d, "sgw")
            sgb = A_B.alloc([8, 128])
            f.dma("sp", sgb, row_d[:, RW_SGB:RW_SGB + 1024].partition_broadcast(128), "rwc")
            ug = [A_B.alloc([1024]) for _ in range(2)]
            tmpg = A_B.alloc([1024])
            wu = None
            for g in range(8):
                if g % 4 == 0:
                    wu = WS.get()
                bu = (g % 2) * 4
                for half in range(2):
                    for kk in range(KC):
                        f.mm(bank(bu + half), wu[:, kk, (g % 4) * 128:(g % 4 + 1) * 128],
                             hT[:, kk, half * 512:(half + 1) * 512], start=(kk == 0), stop=(kk == KC - 1))
                f.act(ug[g % 2], bank(bu, 2), AF.Gelu_apprx_tanh)
                pm = bank(bu + 2, 2)
                for n in range(NT):
                    f.mm(pm[:, n * 128:(n + 1) * 128], svn[:, n, g * 128:(g + 1) * 128], sgwT[:, g, :])
                f.tt("dve", tmpg.rearrange("p (n i) -> p n i", n=8), pm.rearrange("p (n i) -> p n i", n=8),
                     sgb[:, g, :].unsqueeze(1).broadcast_to([128, 8, 128]), ALU.add)
                f.tt("pool", oT[:, 8 + g, :], tmpg, ug[g % 2], ALU.mult)
            if u == 0:
                dump("oTb", oT[:, 8:16, 0:256])
            A_B.release(mC1)
            if stop_after == "SGU":
                return

            heads(u, cst1, GA, BET, BEG, EK, GLB, nseq, L)
            if u == 0:
                dump("oTa", oT[:, 0:8, 0:256])
            if stop_after == "HEADS":
                return

            A_S.release(0)
            xq = [A_S.alloc([512]) for _ in range(3)]
            i = 0
            wo = None
            for c in range(KC):
                if c % 4 == 0:
                    wo = WS.get()
                for half in range(2):
                    xt = xq[i % 3]
                    f.dma("sp", xt, xT_d[u, c * 128:(c + 1) * 128, half * 512:(half + 1) * 512], "xq%d" % (i % 3))
                    pb = bank(4 + i % 4)
                    for kk in range(KC):
                        f.mm(pb, wo[:, kk, (c % 4) * 128:(c % 4 + 1) * 128], oT[:, kk, half * 512:(half + 1) * 512],
                             start=(kk == 0), stop=(kk == KC - 1))
                    f.stt("dve", XR[:, c, half * 512:(half + 1) * 512], pb, MODC[:, u, 2, c:c + 1], xt, ALU.mult, ALU.add)
                    i += 1
            if u == 0:
                dump("x1T", XR[:, :, 0:256])
            if stop_after == "D":
                return
            A_S.release(0)
            rr = rstd_rows(XR, A_S)
            modulate_to(u, 1, XR, rr, A_S, hT)
            h2T = hT
            if stop_after == "D2":
                return

            A_O.release(0)
            A_S.release(0)
            actT = A_O.alloc([JG, 1024], BF16)
            PADW = 4 * 258 if u == 0 else 18 * 66
            raws = [A_O.alloc([PADW]) for _ in range(2)]
            for r in raws:
                f.memset("pool", r, 0.0)
            accs2 = [[A_S.alloc([1024]) for _ in range(2)] for _ in range(2)]
            wds = [A_S.alloc([JG, 256], BF16) for _ in range(2)]
            WDs = WStream(f, wds, "wd")
            for grp_ in range(NG):
                for cb_ in range(8):
                    WDs.plan(w_dn_d[grp_ * JG * 128:(grp_ + 1) * JG * 128, cb_ * 256:(cb_ + 1) * 256]
                             .rearrange("(c p) n -> p c n", p=128))
            fw_ = prm[:, PR_FW:PR_FW + 792].rearrange("p (c t) -> p c t", c=88)
            fb_ = prm[:, PR_FB:PR_FB + 88]

            def views(raw, acc):
                if u == 0:
                    return raw.rearrange("p (s l) -> p s l", s=4), acc.rearrange("p (s l) -> p s l", s=4)
                return raw.rearrange("p (r c) -> p r c", r=18), acc.rearrange("p (r c) -> p r c", r=16)

            def win(rv, a, b):
                if u == 0:
                    return rv[:, :, b:b + 256]
                return rv[:, a:a + 16, b:b + 64]

            def inner(rv):
                if u == 0:
                    return rv[:, :, 1:257]
                return rv[:, 1:17, 1:65]

            taps = [(1, 0), (1, 1), (1, 2)] if u == 0 else [(a, b) for a in range(3) for b in range(3)]
            upctr = 0
            dctr = 0
            ectr = 0
            wu_ = None
            def emit_down(grp):
                nonlocal ectr
                for cb in range(8):
                    wd = WDs.get()
                    for dc in range(2):
                        c = cb * 2 + dc
                        for half in range(2):
                            pb = bank(6 + (ectr % 2))
                            ectr += 1
                            for jj_ in range(JG):
                                f.mm(pb, wd[:, jj_, dc * 128:(dc + 1) * 128], actT[:, jj_, half * 512:(half + 1) * 512],
                                     start=(jj_ == 0), stop=(jj_ == JG - 1))
                            xs = XR[:, c, half * 512:(half + 1) * 512]
                            f.stt("dve", xs, pb, MODC[:, u, 5, c:c + 1], xs, ALU.mult, ALU.add)

            deferred = []
            for j in range(NJ):
                grp, jj = divmod(j, JG)
                if j % 2 == 0:
                    wu_ = WS.get()
                accs = accs2[j % 2]
                for t2 in range(2):
                    cj = 2 * j + t2
                    col0 = (j % 2) * 256 + t2 * 128
                    b0 = (upctr % 3) * 2
                    upctr += 1
                    for half in range(2):
                        for kk in range(KC):
                            f.mm(bank(b0 + half), wu_[:, kk, col0:col0 + 128],
                                 h2T[:, kk, half * 512:(half + 1) * 512], start=(kk == 0), stop=(kk == KC - 1))
                    rv, av = views(raws[t2], accs[t2])
                    src = bank(b0, 2)
                    if u == 0:
                        src = src.rearrange("p (s l) -> p s l", s=4)
                    else:
                        src = src.rearrange("p (r c) -> p r c", r=16)
                    f.copy("act", inner(rv), src)
                    a, b = taps[0]
                    f.act(av, win(rv, a, b), AF.Identity, scale=fw_[:, cj, a * 3 + b:a * 3 + b + 1], bias=fb_[:, cj:cj + 1])
                for ti, (a, b) in enumerate(taps):
                    if ti == 0:
                        continue
                    for t2 in range(2):
                        cj = 2 * j + t2
                        rv, av = views(raws[t2], accs[t2])
                        f.stt("dve", av, win(rv, a, b), fw_[:, cj, a * 3 + b:a * 3 + b + 1], av, ALU.mult, ALU.add)
                f.act(accs[0], accs[0], AF.Silu)
                mult = (lambda jj_, a_: (lambda: f.tt("dve" if u == 1 else "pool", actT[:, jj_, :], a_[0], a_[1], ALU.mult)))(jj, accs)
                if grp > 0 and jj < 2:
                    deferred.append(mult)
                    if jj == 1:
                        emit_down(grp - 1)
                        for m_ in deferred:
                            m_()
                        deferred = []
                else:
                    mult()
                if u == 0 and j == 0:
                    dump("act0", actT[:, 0, 0:256])
            emit_down(NG - 1)
            if u == 0:
                dump("x2T", XR[:, :, 0:256])

            A_O.release(0)
            A_S.release(0)
            rr = rstd_rows(XR, A_S)
            for c in range(KC):
                f.stt("dve", XR[:, c, :], XR[:, c, :], prm[:, PR_FN + c:PR_FN + c + 1], rr, ALU.mult, ALU.mult)
            ys = [A_O.alloc([2048]) for _ in range(2)]
            for m in range(NT):
                pb = bank(4 * (m % 2), 4)
                for c in range(KC):
                    f.tr(pb[:, c * 128:(c + 1) * 128], XR[:, c, m * 128:(m + 1) * 128], ident)
                f.copy("act", ys[m % 2][:, 0:1024], pb[:, 0:1024])
                f.copy("dve", ys[m % 2][:, 1024:2048], pb[:, 1024:2048])
                f.dma("sp", y_d[u, m * 128:(m + 1) * 128, :], ys[m % 2], "ys%d" % (m % 2))

        def heads(u, cst1, GA, BET, BEG, EK, GLB, nseq, L):
            ca = prm[:, PR_CA:PR_CA + 120].rearrange("p (h t k) -> p h t k", h=8, t=3)
            gnw = prm[:, PR_GN:PR_GN + 1]
            QKV = [[A_B.alloc([1024], BF16), A_B.alloc([1024]), A_B.alloc([1024]), A_B.alloc([1024], BF16)] for _ in range(2)]
            PADW = nseq * (L + 4)
            raw = A_B.alloc([PADW])
            f.memset("pool", raw, 0.0)
            rawv = raw.rearrange("p (s l) -> p s l", s=nseq)
            acc = A_B.alloc([1024])
            accv = acc.rearrange("p (s l) -> p s l", s=nseq)
            rn = A_B.alloc([1024])
            osums = [A_B.alloc([1024]) for _ in range(2)]
            rnf = A_B.alloc([1024])
            zeroS = A_B.alloc([128])
            f.memset("pool", zeroS, 0.0)
            Sst = [[A_B.alloc([128]) for _ in range(2)] for _ in range(2)]
            Sbf = [[A_B.alloc([128], BF16) for _ in range(2)] for _ in range(2)]
            sout = [A_B.alloc([128]) for _ in range(2)]
            NM = (cst1[:, C_NMF:C_NMF + 256].rearrange("p (a b) -> p a b", a=2),
                  cst1[:, C_NMB:C_NMB + 256].rearrange("p (a b) -> p a b", a=2))

            def prep_bufs():
                return dict(E=A_B.alloc([2, 128]), R0=A_B.alloc([256]), W=A_B.alloc([3, 128]), eg=A_B.alloc([128]))

            def scan_bufs():
                return dict(AT=A_B.alloc([128], BF16), qd=A_B.alloc([128], BF16), kd=A_B.alloc([128], BF16),
                            u=A_B.alloc([128]), wT=A_B.alloc([128], BF16), vn=A_B.alloc([128], BF16))
            PB = [prep_bufs() for _ in range(4)]
            SB = [[scan_bufs() for _ in range(4)] for _ in range(2)]

            def proj_gen(h, dst):
                wh = WS.get()
                yield
                for t in range(4):
                    for half in range(2):
                        for kk in range(KC):
                            f.mm(bank(half), wh[:, kk, t * 128:(t + 1) * 128], hT[:, kk, half * 512:(half + 1) * 512],
                                 start=(kk == 0), stop=(kk == KC - 1))
                            if kk % 4 == 3:
                                yield
                    src = bank(0, 2)
                    if t == 3:
                        f.act(oT[:, h, :], src, AF.Silu)
                        yield
                        continue
                    f.copy("act", rawv[:, :, 2:2 + L], src.rearrange("p (s l) -> p s l", s=nseq))
                    yield
                    for tap in range(5):
                        wsc = ca[:, h, t, tap:tap + 1]
                        if tap == 0:
                            f.ts("dve", accv, rawv[:, :, 0:L], wsc, None, op0=ALU.mult)
                        else:
                            f.stt("dve", accv, rawv[:, :, tap:tap + L], wsc, accv, ALU.mult, ALU.add)
                        yield
                    if t == 2:
                        f.act(dst[2], acc, AF.Silu)
                        yield
                        continue
                    f.act(acc, acc, AF.Silu)
                    yield
                    f.act(rn, acc, AF.Square)
                    yield
                    for half in range(2):
                        f.mm(bank(half), ones, rn[:, half * 512:(half + 1) * 512])
                        yield
                        f.act(rn[:, half * 512:(half + 1) * 512], bank(half), AF.Ln, bias=epsT)
                        yield
                    f.act(rn, rn, AF.Exp, scale=-0.5)
                    yield
                    if t == 0:
                        f.stt("dve", dst[0], acc, float(128 ** -0.5), rn, ALU.mult, ALU.mult)
                    else:
                        f.tt("dve", dst[1], acc, rn, ALU.mult)
                        yield
                        f.copy("act", dst[3], dst[1])
                    yield

            def prep(h, m, d, qkv, B, S_, ci):
                cd = d * 8 + h
                qT = qkv[0][:, m * 128:(m + 1) * 128]
                kT = qkv[1][:, m * 128:(m + 1) * 128]
                vT = qkv[2][:, m * 128:(m + 1) * 128]
                kTb = qkv[3][:, m * 128:(m + 1) * 128]
                bk = bank(4 + ci)
                slot = [bk[:, i * 128:(i + 1) * 128] for i in range(4)]
                kqkk = bk[:, 0:256].rearrange("p (a b) -> p a b", a=2)
                ktok, vtok = slot[2], slot[3]
                f.mm(kqkk[:, 0, :], kTb, qT)
                f.mm(kqkk[:, 1, :], kTb, kTb)
                f.tr(ktok, kT, ident)
                f.tr(vtok, vT, ident)
                yield
                gc = GA[:, m, cd, 0:1]
                E = B["E"]
                f.tt("dve", E, ident.unsqueeze(1).broadcast_to([128, 2, 128]),
                     GA[:, m, cd, :].unsqueeze(2).broadcast_to([128, 2, 128]), ALU.mult)
                f.act(B["R0"][:, 0:128], vtok, AF.Identity, scale=BET[:, m, cd:cd + 1])
                yield
                f.act(B["R0"][:, 128:256], ktok, AF.Identity, scale=BEG[:, m, cd:cd + 1])
                f.ts("dve", S_["kd"], ktok, EK[:, m, cd:cd + 1], None, op0=ALU.mult)
                yield
                rows = bk[:, 256:512]
                f.mm(rows, ones, E.rearrange("p a b -> p (a b)"))
                yield
                rows3 = rows.rearrange("p (a b) -> p a b", a=2)
                f.stt("dve", E, rows3, gc, NM[d], ALU.subtract, ALU.min)
                f.act(B["eg"], rows3[:, 0, :], AF.Exp)
                yield
                f.act(E, E, AF.Exp)
                f.tt("pool", S_["qd"], B["eg"], qT, ALU.mult)
                yield
                W = B["W"]
                NT_ = W[:, 1, :]
                f.tt("dve", NT_, kqkk[:, 1, :], E[:, 1, :], ALU.mult)
                f.tt("dve", S_["AT"], kqkk[:, 0, :], E[:, 0, :], ALU.mult)
                yield
                f.tr(slot[0], NT_, ident)
                f.tt("pool", W[:, 0, :], ident, NT_, ALU.subtract)
                yield
                f.copy("act", W[:, 2, :], slot[0])
                yield
                f.mm(slot[1], W[:, 2, :], W[:, 1, :])
                f.mm(slot[2], W[:, 1, :], W[:, 2, :])
                yield
                f.copy("act", W[:, 1:3, :], bk[:, 128:384].rearrange("p (a b) -> p a b", a=2))
                yield
                for lev in range(1, 5):
                    f.mm(bk[:, 0:256], W[:, 2, :], W[:, 0:2, :].rearrange("p a b -> p (a b)"))
                    f.mm(slot[2], W[:, 1, :], W[:, 2, :])
                    yield
                    f.tt("dve", W[:, 0, :], W[:, 0, :], slot[0], ALU.add)
                    f.copy("act", W[:, 1:3, :], bk[:, 128:384].rearrange("p (a b) -> p a b", a=2))
                    yield
                f.mm(slot[0], W[:, 2, :], W[:, 0, :])
                yield
                f.tt("dve", W[:, 0, :], W[:, 0, :], slot[0], ALU.add)
                yield
                Tt = W[:, 0, :]
                f.mm(slot[1], Tt, B["R0"][:, 0:128])
                f.mm(slot[2], B["R0"][:, 128:256], Tt)
                yield
                f.copy("act", S_["u"], slot[1])
                f.copy("dve", S_["wT"], slot[2])
                yield

            def init_state(d, h, idx):
                dst = Sst[d][idx]
                if u == 1:
                    f.dma("sp", dst, s0_d[:, d, h, :], "s0%d" % d)
                else:
                    f.copy("pool", dst, zeroS)
                f.copy("act", Sbf[d][idx], dst)

            Sidx = [0, 0]

            def scan_chain(h, d, tiles, Bs):
                cd = d * 8 + h
                bS = bank(2 + d)
                vnp = bS[:, 0:128]
                otp = bS[:, 128:192]
                for m, B in zip(tiles, Bs):
                    for ci in range(2):
                        c = ci if d == 0 else 1 - ci
                        lo, hi = c * 64, (c + 1) * 64
                        if u == 0:
                            first = (m % 2 == 0 and c == 0) if d == 0 else (m % 2 == 1 and c == 1)
                            if first:
                                init_state(d, h, Sidx[d])
                        Scur, Snew = Sst[d][Sidx[d]], Sst[d][1 - Sidx[d]]
                        Sbc, Sbn = Sbf[d][Sidx[d]], Sbf[d][1 - Sidx[d]]
                        f.mm(vnp, B["wT"], Sbc)
                        yield
                        f.tt("dve", B["vn"][lo:hi, :], B["u"][lo:hi, :], vnp[lo:hi, :], ALU.subtract)
                        yield
                        f.mm(otp, Sbc, B["qd"][:, lo:hi], start=True, stop=False)
                        f.mm(otp, B["vn"][lo:hi, :], B["AT"][lo:hi, lo:hi], start=False, stop=True)
                        f.mm(vnp, B["kd"][lo:hi, :], B["vn"][lo:hi, :])
                        yield
                        t0 = m * 128 + lo
                        f.stt("dve", Snew, Scur, GLB[:, m, c, cd:cd + 1], vnp, ALU.mult, ALU.add)
                        osum = osums[h % 2]
                        f.tt("dve", osum[:, t0:t0 + 64], osum[:, t0:t0 + 64], otp, ALU.add)
                        yield
                        f.copy("act", Sbn, Snew)
                        yield
                        Sidx[d] = 1 - Sidx[d]
                        if u == 0:
                            last = (m % 2 == 1 and c == 1) if d == 0 else (m % 2 == 0 and c == 0)
                            if last:
                                seq = m // 2
                                f.copy("pool", sout[d], Sst[d][Sidx[d]])
                                dst = (nsf_d if d == 0 else nsb_d)[seq, h, :, :]
                                f.dma("sp", dst, sout[d], "so%d" % d)

            def finalize(h):
                osum = osums[h % 2]
                f.act(rnf, osum, AF.Square)
                yield
                for half in range(2):
                    f.mm(bank(half), ones, rnf[:, half * 512:(half + 1) * 512])
                    yield
                    f.act(rnf[:, half * 512:(half + 1) * 512], bank(half), AF.Ln, bias=epsT, scale=1.0 / 128)
                    yield
                f.act(rnf, rnf, AF.Exp, scale=-0.5)
                yield
                f.stt("dve", rnf, osum, gnw, rnf, ALU.mult, ALU.mult)
                yield
                f.tt("dve", oT[:, h, :], rnf, oT[:, h, :], ALU.mult)
                if u == 0 and h == 0:
                    dump("os0", osum[:, 0:256])
                yield

            def run_rr(chains):
                act_ = list(chains)
                while act_:
                    for g in list(act_):
                        try:
                            next(g)
                        except StopIteration:
                            act_.remove(g)

            def limited(g, n):
                for _ in range(n):
                    try:
                        next(g)
                    except StopIteration:
                        return
                    yield

            def tiles_of(r):
                return [(0, 2 * r), (0, 2 * r + 1), (1, 7 - 2 * r), (1, 6 - 2 * r)]

            g0 = proj_gen(0, QKV[0])
            for _ in g0:
                pass
            gen_next = None
            fin_pending = None
            NR = 4
            for R in range(H * NR + 1):
                chains = []
                if R < H * NR:
                    h, r = divmod(R, NR)
                    if r == 0:
                        if u == 0 and h == 0:
                            dump("q0", QKV[0][0][:, 0:256])
                            dump("k0", QKV[0][1][:, 0:256])
                            dump("v0", QKV[0][2][:, 0:256])
                        if h + 1 < H:
                            gen_next = proj_gen(h + 1, QKV[(h + 1) % 2])
                    for ci, (d, m) in enumerate(tiles_of(r)):
                        chains.append(prep(h, m, d, QKV[h % 2], PB[ci], SB[R % 2][ci], ci))
                    if gen_next is not None:
                        chains.append(gen_next if r == NR - 1 else limited(gen_next, 20))
                if R >= 1:
                    h2, r2 = divmod(R - 1, NR)
                    if r2 == 0:
                        f.memset("pool", osums[h2 % 2], 0.0)
                        for d in range(2):
                            Sidx[d] = 0
                            if u == 1:
                                init_state(d, h2, 0)
                    tl = tiles_of(r2)
                    Bp = SB[(R - 1) % 2]
                    chains.append(scan_chain(h2, 0, [tl[0][1], tl[1][1]], [Bp[0], Bp[1]]))
                    chains.append(scan_chain(h2, 1, [tl[2][1], tl[3][1]], [Bp[2], Bp[3]]))
                if fin_pending is not None:
                    chains.append(fin_pending)
                    fin_pending = None
                run_rr(chains)
                if R >= 1 and (R - 1) % NR == NR - 1:
                    fin_pending = finalize((R - 1) // NR)
            for _ in fin_pending:
                pass

        phase_A()
        dump("modc", MODC)
        if stop_after != "A":
            for u in units:
                unit(u)
        f.emit()
        nc._arena_peak = (AP_.peak, A_B.peak, A_S.peak, A_O.peak)
    return nc


_PROG = {}


def _col(v, n):
    return np.ascontiguousarray(np.asarray(v, np.float32).reshape(n, 128).T)


def prep_inputs(inp, cores=range(8)):
    f32 = lambda a: np.asarray(a, np.float32)
    w_in = f32(inp["w_in"])[0]
    perm = list(range(5152, 6176)) + list(range(4096, 4128))
    for h in range(H):
        for t in range(4):
            perm += list(range(t * 1024 + h * 128, t * 1024 + (h + 1) * 128))
    perm += list(range(4128, 5152))
    w_in_r = np.ascontiguousarray(w_in[:, perm])
    permu = []
    for j in range(NJ):
        permu += list(range(j * 128, (j + 1) * 128)) + list(range(DFF + j * 128, DFF + (j + 1) * 128))
    w_up_r = np.ascontiguousarray(f32(inp["w_up"])[0][:, permu])
    fcw = f32(inp["ffn_conv_w"])[0].reshape(9, 2 * DFF)[:, permu]
    fcw = np.ascontiguousarray(fcw.reshape(9, 88, 128).transpose(2, 1, 0))
    fcb = np.ascontiguousarray(f32(inp["ffn_conv_b"])[0][permu].reshape(88, 128).T)
    caw = f32(inp["conv_a_w"])[0, 0]
    caw = np.ascontiguousarray(caw.reshape(5, 3, 8, 128).transpose(3, 2, 1, 0))
    c0, c1 = _make_consts()
    shared = dict(
        cst0=c0, cst1=c1,
        w_ada=np.ascontiguousarray(f32(inp["w_ada"])[0]),
        b_ada=np.ascontiguousarray(f32(inp["b_ada"])[0][None, :]),
        w_in=w_in_r,
        w_out=np.ascontiguousarray(f32(inp["w_out"])[0]),
        w_up=w_up_r,
        w_down=np.ascontiguousarray(f32(inp["w_down"])[0]),
        sgw=np.ascontiguousarray(f32(inp["sgu_w"])[0].transpose(2, 0, 1)),
    )
    row = np.zeros((1, NROW), np.float32)
    row[0, RW_DTB:RW_DTB + 16] = f32(inp["dt_bias"])[0].reshape(16)
    row[0, RW_ALOG:RW_ALOG + 16] = f32(inp["a_log"])[0].reshape(16)
    row[0, RW_SNW:RW_SNW + 1024] = f32(inp["sgu_norm_w"])[0]
    row[0, RW_SGB:RW_SGB + 1024] = f32(inp["sgu_b"])[0].reshape(1024)
    shared["rowp"] = row
    prm0 = np.zeros((128, NPRM), np.float32)
    prm0[:, PR_N1:PR_N1 + 16] = _col(inp["norm1_w"][0], 16)
    prm0[:, PR_N2:PR_N2 + 16] = _col(inp["norm2_w"][0], 16)
    prm0[:, PR_FN:PR_FN + 16] = _col(inp["final_norm_w"], 16)
    prm0[:, PR_GN] = f32(inp["gdn_norm_w"])[0]
    prm0[:, PR_CA:PR_CA + 120] = caw.reshape(128, 120)
    prm0[:, PR_FW:PR_FW + 792] = fcw.reshape(128, 792)
    prm0[:, PR_FB:PR_FB + 88] = fcb
    xp = f32(inp["x_prompt"])
    xs = f32(inp["x_sample"])
    sf = f32(inp["state_fwd"])
    sb = f32(inp["state_bwd"])
    cc = f32(inp["c"])
    cctx = f32(inp["c_ctx"])
    maps = []
    for i in cores:
        m = dict(shared)
        xc = np.stack([xp[4 * i:4 * i + 4].reshape(T, D), xs[i]], axis=0)
        m["xT"] = np.ascontiguousarray(xc.transpose(0, 2, 1))
        prm = prm0.copy()
        prm[:, PR_C:PR_C + 16] = _col(cctx, 16)
        prm[:, PR_C + 16:PR_C + 32] = _col(cc[i], 16)
        m["prm"] = prm
        s0 = np.stack([sf[i, 0], sb[i, 0]], axis=0)
        m["s0"] = np.ascontiguousarray(s0.transpose(2, 0, 1, 3))
        maps.append(m)
    return maps


def kernel(**inputs):
    if "nc" not in _PROG:
        _PROG["nc"] = build_program()
    nc = _PROG["nc"]
    maps = prep_inputs(inputs)
    res = run_bass_kernel_spmd(nc, maps, core_ids=list(range(8)))
    r = res.results
    y_prompt = np.concatenate([r[i]["y"][0].reshape(4, 256, D) for i in range(8)], axis=0).astype(np.float32)
    y_sample = np.stack([r[i]["y"][1] for i in range(8)], axis=0).astype(np.float32)
    nsf = np.concatenate([r[i]["nsf"][:, None] for i in range(8)], axis=0).astype(np.float32)
    nsb = np.concatenate([r[i]["nsb"][:, None] for i in range(8)], axis=0).astype(np.float32)
    return (y_prompt, y_sample, nsf, nsb)
```
